# Optimizing a Trainium2 kernel written in Bass

```python
import math
import jax, jax.numpy as jnp
from jax import lax
import numpy as np

D_MODEL = 2048
BATCH = 16
SEQ = 256
DEPTH = 4
DEC_BATCH = 8
DEC_SEQ = 1024
PAST_LEN = 256

GRID_W = 64
HEAD_DIM = 128
NA_HEADS = 4
GQA_Q_HEADS = 8
GQA_KV_HEADS = 2
GQA_GROUP = GQA_Q_HEADS // GQA_KV_HEADS
DIFF_HEADS = 4
DIFF_QK_DIM = HEAD_DIM // 2
NA_KH = 8
NA_KW = 16
NA_QC = NA_KW
NA_CB = 2 * NA_KW
Q_BLOCK = 128
D_FF = 4 * D_MODEL
ROPE_THETA = 10000.0
EPS = 1e-6
N_MOD = 6
NEG_BIG = -1e30

NA_W = NA_HEADS * HEAD_DIM
GQA_QW = GQA_Q_HEADS * HEAD_DIM
GQA_KVW = GQA_KV_HEADS * HEAD_DIM
DIFF_W = DIFF_HEADS * HEAD_DIM
D_IN = 3 * NA_W + GQA_QW + 2 * GQA_KVW + 3 * DIFF_W
D_MIX = NA_W + GQA_QW + DIFF_W
IN_SPLITS = (NA_W, 2 * NA_W, 3 * NA_W,
             3 * NA_W + GQA_QW,
             3 * NA_W + GQA_QW + GQA_KVW,
             3 * NA_W + GQA_QW + 2 * GQA_KVW,
             3 * NA_W + GQA_QW + 2 * GQA_KVW + DIFF_W,
             3 * NA_W + GQA_QW + 2 * GQA_KVW + 2 * DIFF_W)

kernel_name = "hybrid_diffusion_prefix_trunk_step"


def rms_norm(x, g):
    xf = x.astype(jnp.float32)
    y = xf * lax.rsqrt(jnp.mean(xf * xf, axis=-1, keepdims=True) + EPS)
    return (y * g.astype(jnp.float32)).astype(x.dtype)


def rope_1d(x, pos):
    T = x.shape[1]
    half = x.shape[-1] // 2
    inv = ROPE_THETA ** (-jnp.arange(half, dtype=jnp.float32) / half)
    ang = pos.astype(jnp.float32)[:, None] * inv[None, :]
    bshape = (T,) + (1,) * (x.ndim - 3) + (half,)
    cos = jnp.cos(ang).reshape(bshape)
    sin = jnp.sin(ang).reshape(bshape)
    x1 = x[..., :half].astype(jnp.float32)
    x2 = x[..., half:].astype(jnp.float32)
    return jnp.concatenate([x1 * cos - x2 * sin, x2 * cos + x1 * sin], axis=-1).astype(x.dtype)


def axial_rope(x):
    T = x.shape[1]
    t = jnp.arange(T)
    d = x.shape[-1]
    return jnp.concatenate([rope_1d(x[..., : d // 2], t // GRID_W),
                            rope_1d(x[..., d // 2:], t % GRID_W)], axis=-1)


def sweep_query_blocks(fn, q):
    B, T = q.shape[:2]
    nb = T // Q_BLOCK
    qb = q.reshape((B, nb, Q_BLOCK) + q.shape[2:]).swapaxes(0, 1)
    out = lax.map(fn, qb)
    return out.swapaxes(0, 1).reshape((B, T) + out.shape[3:])


def dense_attention(q, k, v):
    scale = q.shape[-1] ** -0.5
    def block(qb):
        s = jnp.einsum('bqkgd,bskd->bkgqs', qb, k).astype(jnp.float32) * scale
        p = jax.nn.softmax(s, axis=-1).astype(v.dtype)
        return jnp.einsum('bkgqs,bskd->bqkgd', p, v)
    return sweep_query_blocks(block, q)


def diff_attention(q, k, v, lam):
    scale = q.shape[-1] ** -0.5
    def block(qb):
        s = jnp.einsum('bqhmd,bshmd->bhmqs', qb, k).astype(jnp.float32) * scale
        p = jax.nn.softmax(s, axis=-1)
        a = (p[:, :, 0] - lam * p[:, :, 1]).astype(v.dtype)
        return jnp.einsum('bhqs,bshd->bqhd', a, v)
    return sweep_query_blocks(block, q)


def diff_lambda(lam_vec, lambda_init):
    lf = lam_vec.astype(jnp.float32)
    return jnp.exp(jnp.sum(lf[0] * lf[1])) - jnp.exp(jnp.sum(lf[2] * lf[3])) + lambda_init


def neighbourhood_attention(q, k, v, k_ctx, v_ctx, rpb):
    B, T, H, d = q.shape
    rows = T // GRID_W
    kh = min(NA_KH, rows)
    ncb = GRID_W // NA_QC
    scale = d ** -0.5
    r = jnp.arange(rows)
    r0 = jnp.clip(r - kh // 2, 0, rows - kh)
    row_idx = r0[:, None] + jnp.arange(kh)[None, :]
    j = jnp.arange(ncb)
    c0 = jnp.clip(j * NA_QC - NA_KW // 2, 0, GRID_W - NA_CB)
    col_idx = c0[:, None] + jnp.arange(NA_CB)[None, :]
    qc = j[:, None] * NA_QC + jnp.arange(NA_QC)[None, :]
    ws = jnp.clip(qc - NA_KW // 2, 0, GRID_W - NA_KW)
    ri = row_idx[:, None, :, None]
    ci = col_idx[None, :, None, :]
    n_loc = kh * NA_CB
    kg = k.reshape(B, rows, GRID_W, H, d)[:, ri, ci].reshape(B, rows, ncb, n_loc, H, d)
    vg = v.reshape(B, rows, GRID_W, H, d)[:, ri, ci].reshape(B, rows, ncb, n_loc, H, d)
    cand = col_idx[:, None, :]
    valid = (cand >= ws[:, :, None]) & (cand < ws[:, :, None] + NA_KW)
    drow = (row_idx - r[:, None] + NA_KH - 1).reshape(rows, 1, 1, kh, 1)
    dcol = jnp.clip(cand - qc[:, :, None] + NA_KW - 1, 0, 2 * NA_KW - 2).reshape(1, ncb, NA_QC, 1, NA_CB)
    bias = rpb.astype(jnp.float32)[:, drow, dcol]
    bias = jnp.where(valid[None, None, :, :, None, :], bias, NEG_BIG)
    bias = bias.reshape(H, rows, ncb, NA_QC, n_loc).transpose(1, 2, 0, 3, 4)
    qg = q.reshape(B, rows, ncb, NA_QC, H, d)
    s_loc = jnp.einsum('brjqhd,brjnhd->brjhqn', qg, kg).astype(jnp.float32) * scale + bias
    s_ctx = jnp.einsum('brjqhd,blhd->brjhql', qg, k_ctx).astype(jnp.float32) * scale
    p = jax.nn.softmax(jnp.concatenate([s_loc, s_ctx], axis=-1), axis=-1).astype(v.dtype)
    o = (jnp.einsum('brjhqn,brjnhd->brjqhd', p[..., :n_loc], vg)
         + jnp.einsum('brjhql,blhd->brjqhd', p[..., n_loc:], v_ctx))
    return o.reshape(B, T, H, d)


def modulation(cvec, w_ada_l, b_ada_l):
    m = jax.nn.silu(cvec) @ w_ada_l + b_ada_l
    return jnp.split(m[:, None, :], N_MOD, axis=-1)


def project(h, w_in_l):
    B, T, _ = h.shape
    parts = jnp.split(h @ w_in_l, IN_SPLITS, axis=-1)
    return [p.reshape(B, T, -1, HEAD_DIM) for p in parts]


def split_maps(a):
    return a.reshape(a.shape[:-1] + (2, DIFF_QK_DIM))


def merge_heads(o_na, o_gqa, o_diff, diff_g_l, lambda_init, w_out_l):
    B, T = o_na.shape[:2]
    o_diff = rms_norm(o_diff, diff_g_l) * (1.0 - lambda_init)
    o = jnp.concatenate([o_na.reshape(B, T, NA_W), o_gqa.reshape(B, T, GQA_QW),
                         o_diff.reshape(B, T, DIFF_W)], axis=-1)
    return o @ w_out_l


def mix_context(h, w_in_l, w_out_l, q_g, k_g, lam_vec, diff_g_l, lambda_init):
    B, T, _ = h.shape
    na_q, na_k, na_v, g_q, g_k, g_v, d_q, d_k, d_v = project(h, w_in_l)
    g_q = rms_norm(g_q, q_g)
    g_k = rms_norm(g_k, k_g)
    o_na = dense_attention(na_q[:, :, :, None], na_k, na_v)
    o_gqa = dense_attention(g_q.reshape(B, T, GQA_KV_HEADS, GQA_GROUP, HEAD_DIM), g_k, g_v)
    lam = diff_lambda(lam_vec, lambda_init)
    o_diff = diff_attention(split_maps(d_q), split_maps(d_k), d_v, lam)
    o = merge_heads(o_na, o_gqa, o_diff, diff_g_l, lambda_init, w_out_l)
    return o, (na_k, na_v, g_k, g_v, d_k, d_v)


def mix_latent(h, ctx, w_in_l, w_out_l, rpb_l, q_g, k_g, lam_vec, diff_g_l, lambda_init):
    B, T, _ = h.shape
    na_kc, na_vc, g_kc, g_vc, d_kc, d_vc = ctx
    na_q, na_k, na_v, g_q, g_k, g_v, d_q, d_k, d_v = project(h, w_in_l)
    o_na = neighbourhood_attention(na_q, na_k, na_v, na_kc, na_vc, rpb_l)
    g_q = axial_rope(rms_norm(g_q, q_g))
    g_k = axial_rope(rms_norm(g_k, k_g))
    o_gqa = dense_attention(g_q.reshape(B, T, GQA_KV_HEADS, GQA_GROUP, HEAD_DIM),
                            jnp.concatenate([g_k, g_kc], axis=1),
                            jnp.concatenate([g_v, g_vc], axis=1))
    lam = diff_lambda(lam_vec, lambda_init)
    o_diff = diff_attention(axial_rope(split_maps(d_q)),
                            jnp.concatenate([axial_rope(split_maps(d_k)), split_maps(d_kc)], axis=1),
                            jnp.concatenate([d_v, d_vc], axis=1), lam)
    return merge_heads(o_na, o_gqa, o_diff, diff_g_l, lambda_init, w_out_l)


def squared_relu_mlp(h, w_up_l, w_down_l):
    return jnp.square(jax.nn.relu(h @ w_up_l)) @ w_down_l


def setup_inputs(seed: int = 0) -> dict:
    key = jax.random.key(seed)
    ks = jax.random.split(key, 22)
    f32 = jnp.float32
    def nrm(k, shape, s):
        return s * jax.random.normal(k, shape, f32)
    return {
        "x_prompt": nrm(ks[0], (BATCH, SEQ, D_MODEL), 1.0),
        "x_sample": nrm(ks[1], (DEC_BATCH, DEC_SEQ, D_MODEL), 1.0),
        "c": nrm(ks[2], (DEC_BATCH, D_MODEL), 1.0),
        "cache_na_k": nrm(ks[3], (DEC_BATCH, DEPTH, PAST_LEN, NA_HEADS, HEAD_DIM), 1.0),
        "cache_na_v": nrm(ks[4], (DEC_BATCH, DEPTH, PAST_LEN, NA_HEADS, HEAD_DIM), 1.0),
        "cache_gqa_k": nrm(ks[5], (DEC_BATCH, DEPTH, PAST_LEN, GQA_KV_HEADS, HEAD_DIM), 1.0),
        "cache_gqa_v": nrm(ks[6], (DEC_BATCH, DEPTH, PAST_LEN, GQA_KV_HEADS, HEAD_DIM), 1.0),
        "cache_diff_k": nrm(ks[7], (DEC_BATCH, DEPTH, PAST_LEN, DIFF_HEADS, HEAD_DIM), 1.0),
        "cache_diff_v": nrm(ks[8], (DEC_BATCH, DEPTH, PAST_LEN, DIFF_HEADS, HEAD_DIM), 1.0),
        "c_ctx": nrm(ks[9], (D_MODEL,), 1.0),
        "w_ada": nrm(ks[10], (DEPTH, D_MODEL, N_MOD * D_MODEL), D_MODEL ** -0.5),
        "b_ada": nrm(ks[11], (DEPTH, N_MOD * D_MODEL), 0.02),
        "norm_g": 1.0 + nrm(ks[12], (DEPTH, 4, D_MODEL), 0.02),
        "w_in": nrm(ks[13], (DEPTH, D_MODEL, D_IN), D_MODEL ** -0.5),
        "w_out": nrm(ks[14], (DEPTH, D_MIX, D_MODEL), D_MIX ** -0.5),
        "na_rpb": nrm(ks[15], (DEPTH, NA_HEADS, 2 * NA_KH - 1, 2 * NA_KW - 1), 0.1),
        "gqa_q_g": 1.0 + nrm(ks[16], (DEPTH, HEAD_DIM), 0.02),
        "gqa_k_g": 1.0 + nrm(ks[17], (DEPTH, HEAD_DIM), 0.02),
        "diff_lam": nrm(ks[18], (DEPTH, 4, DIFF_QK_DIM), 0.1),
        "diff_g": 1.0 + nrm(ks[19], (DEPTH, HEAD_DIM), 0.02),
        "w_up": nrm(ks[20], (DEPTH, D_MODEL, D_FF), D_MODEL ** -0.5),
        "w_down": nrm(ks[21], (DEPTH, D_FF, D_MODEL), D_FF ** -0.5),
    }


def reference(x_prompt, x_sample, c, cache_na_k, cache_na_v, cache_gqa_k, cache_gqa_v,
              cache_diff_k, cache_diff_v, c_ctx, w_ada, b_ada, norm_g, w_in, w_out,
              na_rpb, gqa_q_g, gqa_k_g, diff_lam, diff_g, w_up, w_down):
    xp = x_prompt
    xs = x_sample
    new_kv = [[] for _ in range(6)]
    for l in range(DEPTH):
        lambda_init = 0.8 - 0.6 * math.exp(-0.3 * l)
        g_pre_a, g_post_a, g_pre_m, g_post_m = norm_g[l, 0], norm_g[l, 1], norm_g[l, 2], norm_g[l, 3]
        sh_a, sc_a, gt_a, sh_m, sc_m, gt_m = modulation(c_ctx[None, :], w_ada[l], b_ada[l])
        h = rms_norm(xp, g_pre_a) * (1.0 + sc_a) + sh_a
        o, ctx_kv = mix_context(h, w_in[l], w_out[l], gqa_q_g[l], gqa_k_g[l],
                                diff_lam[l], diff_g[l], lambda_init)
        for i in range(6):
            new_kv[i].append(ctx_kv[i])
        xp = xp + gt_a * rms_norm(o, g_post_a)
        h = rms_norm(xp, g_pre_m) * (1.0 + sc_m) + sh_m
        xp = xp + gt_m * rms_norm(squared_relu_mlp(h, w_up[l], w_down[l]), g_post_m)
        sh_a, sc_a, gt_a, sh_m, sc_m, gt_m = modulation(c, w_ada[l], b_ada[l])
        h = rms_norm(xs, g_pre_a) * (1.0 + sc_a) + sh_a
        ctx = (cache_na_k[:, l], cache_na_v[:, l], cache_gqa_k[:, l], cache_gqa_v[:, l],
               cache_diff_k[:, l], cache_diff_v[:, l])
        o = mix_latent(h, ctx, w_in[l], w_out[l], na_rpb[l], gqa_q_g[l], gqa_k_g[l],
                       diff_lam[l], diff_g[l], lambda_init)
        xs = xs + gt_a * rms_norm(o, g_post_a)
        h = rms_norm(xs, g_pre_m) * (1.0 + sc_m) + sh_m
        xs = xs + gt_m * rms_norm(squared_relu_mlp(h, w_up[l], w_down[l]), g_post_m)
    new_na_k = jnp.stack(new_kv[0], axis=1)
    new_na_v = jnp.stack(new_kv[1], axis=1)
    new_gqa_k = jnp.stack(new_kv[2], axis=1)
    new_gqa_v = jnp.stack(new_kv[3], axis=1)
    new_diff_k = jnp.stack(new_kv[4], axis=1)
    new_diff_v = jnp.stack(new_kv[5], axis=1)
    return (xp, xs, new_na_k, new_na_v, new_gqa_k, new_gqa_v, new_diff_k, new_diff_v)
```

```python
import math
from contextlib import ExitStack

import numpy as np
import concourse.bass as bass
import concourse.mybir as mybir
from concourse.bass_utils import run_bass_kernel_spmd

F32 = mybir.dt.float32
BF16 = mybir.dt.bfloat16
AF = mybir.ActivationFunctionType
ALU = mybir.AluOpType

D = 2048
DC = 16
L_FULL = 4
HD = 128
EPS = 1e-6
D_IN = 4608
D_FF = 8192
GRID_W = 64
NEG = -30000.0
SQD = math.sqrt(2048.0)
SQH = math.sqrt(128.0)
SCALE_H = 128.0 ** -0.5
SCALE_DQ = 64.0 ** -0.5
TB_E = 22

C_NAQ, C_NAK, C_NAV = 0, 512, 1024
C_GQ, C_GK, C_GV = 1536, 2560, 2816
C_DQ, C_DK, C_DV = 3072, 3584, 4096


class Op:
    __slots__ = ("eng", "fn", "dma", "dma_val", "signal", "deps", "cnt")


class Sched:
    STREAMS = ("pe", "act", "dve", "pool", "sp")

    def __init__(self):
        self.streams = {k: [] for k in self.STREAMS}
        self.res = {}
        self.dma_cum = {}
        self.bar_deps = {k: [] for k in self.STREAMS}
        self.last_barrier = []
        self.sp_latest = {}
        self.nops = 0

    def add(self, eng, fn, r=(), w=(), dma=None, post_barrier=False):
        op = Op()
        op.eng, op.fn, op.dma, op.signal, op.deps, op.cnt = eng, fn, dma, False, [], 0
        op.dma_val = 0
        cand = []
        for key in r:
            st = self.res.get(key)
            if st is not None and st[0] is not None:
                cand.append((st[0], "raw"))
        for key in w:
            st = self.res.get(key)
            if st is not None:
                if st[0] is not None:
                    cand.append((st[0], "waw"))
                for rd in st[1]:
                    cand.append((rd, "war"))
        for p in self.bar_deps[eng]:
            cand.append((p, "bar"))
        self.bar_deps[eng] = []
        if post_barrier:
            for p in self.last_barrier:
                cand.append((p, "bar"))
        seen = set()
        for p, kind in cand:
            if id(p) in seen:
                continue
            if p.dma is None and p.eng == eng:
                if eng == "pe" and dma is None:
                    continue
            seen.add(id(p))
            if p.dma is not None:
                op.deps.append((p, self.dma_cum[p.dma]))
            else:
                p.signal = True
                op.deps.append((p, None))
        for key in r:
            st = self.res.setdefault(key, [None, []])
            st[1].append(op)
        for key in w:
            self.res[key] = [op, []]
        if dma is not None:
            self.dma_cum[dma] = self.dma_cum.get(dma, 0) + 16
            op.dma_val = self.dma_cum[dma]
            if eng == "sp":
                self.sp_latest[dma] = op
        self.streams[eng].append(op)
        self.nops += 1
        return op

    def barrier(self, engs=("pe", "act", "dve", "sp")):
        lasts = []
        for e in engs:
            if self.streams[e]:
                lasts.append(self.streams[e][-1])
        for k_, p_ in self.sp_latest.items():
            if p_ not in lasts:
                lasts.append(p_)
        self.sp_latest = {}
        for e in engs:
            self.bar_deps[e] = [p for p in lasts if p.eng != e or p.dma is not None]
        self.last_barrier = lasts

    def emit(self, nc, block_engines, eng_sems, dma_sems):
        for name in self.STREAMS:
            cnt = 0
            for op in self.streams[name]:
                if op.dma is None and op.signal:
                    cnt += 1
                op.cnt = cnt

        def run_stream(name):
            def body(e):
                waited = {}
                for op in self.streams[name]:
                    for p, v in op.deps:
                        if p.dma is not None:
                            sem, val, sid = dma_sems[p.dma], v, ("d", p.dma)
                        else:
                            sem, val, sid = eng_sems[p.eng], p.cnt, ("e", p.eng)
                        if waited.get(sid, 0) >= val:
                            continue
                        e.wait_ge(sem, val)
                        waited[sid] = val
                    ins = op.fn(e)
                    if op.dma is not None:
                        ins.then_inc(dma_sems[op.dma], 16)
                    elif op.signal:
                        ins.then_inc(eng_sems[name], 1)
                if name == "sp":
                    for key, val in self.dma_cum.items():
                        if waited.get(("d", key), 0) < val:
                            e.wait_ge(dma_sems[key], val)
            return body

        block_engines["pe"](run_stream("pe"))
        block_engines["act"](run_stream("act"))
        block_engines["dve"](run_stream("dve"))
        block_engines["pool"](run_stream("pool"))
        block_engines["sp"](run_stream("sp"))


class _Stop(Exception):
    pass


class Builder:
    def __init__(self, depth=L_FULL, taps=(), stop=None):
        self.stop = stop
        self.depth = depth
        self.taps = set(taps)
        self.tap_outs = {}
        self.nc = bass.Bass("TRN2", target_bir_lowering=False)
        self.S = Sched()
        self.dma_keys = set()
        self.bank_rr = 0
        self.misc_rr = 0
        self.slab_rr = 0
        self.evac_rr = 0
        self.work_rr = 0
        self.e_rr = 0
        self.rst_rr = 0
        self.ostg_rr = 0

    def din(self, name, shape):
        return self.nc.dram_tensor(name, list(shape), F32, kind="ExternalInput").ap()

    def dout(self, name, shape):
        return self.nc.dram_tensor(name, list(shape), F32, kind="ExternalOutput").ap()

    def op(self, eng, fn, r=(), w=(), **kw):
        return self.S.add(eng, fn, r=r, w=w, **kw)

    def dma(self, eng, out, in_, key, r=(), w=(), post_barrier=False):
        key = eng + ":" + key
        self.dma_keys.add(key)
        return self.S.add(eng, lambda e, out=out, in_=in_: e.dma_start(out=out, in_=in_), r=r, w=w, dma=key,
                          post_barrier=post_barrier)

    def next_bank(self, pool=(0, 1, 2, 3, 4, 5)):
        b = pool[self.bank_rr % len(pool)]
        self.bank_rr += 1
        return b

    def misc_bank(self):
        b = 6 + (self.misc_rr % 2)
        self.misc_rr += 1
        return b

    def work(self):
        i = self.work_rr % len(self.wk)
        self.work_rr += 1
        return self.wk[i], ("wk", i)

    def etile(self):
        i = self.e_rr % len(self.et)
        self.e_rr += 1
        return self.et[i], ("et", i)

    def evac_eng(self):
        self.evac_rr += 1
        return "act" if self.evac_rr % 2 else "dve"

    def copy_op(self, eng, out, in_, r, w):
        if eng == "act":
            return self.op("act", lambda e: e.activation(out=out, in_=in_, func=AF.Copy), r=r, w=w)
        return self.op("dve", lambda e: e.tensor_copy(out=out, in_=in_), r=r, w=w)

    def stage(self, name):
        if self.stop == name:
            raise _Stop()

    def tap(self, name, ap, rkeys, shape):
        if name not in self.taps:
            return
        o = self.nc.dram_tensor("tap_" + name, list(shape), ap.dtype, kind="ExternalOutput").ap()
        self.tap_outs[name] = (shape, ap.dtype)
        self.dma("sp", o, ap, "tap_" + name, r=rkeys)

    def load_slab(self, wl, k0, kc, c0, gw):
        j = self.slab_rr % len(self.slabs)
        self.slab_rr += 1
        slab = self.slabs[j]
        src = wl[k0:k0 + kc * 128, c0:c0 + gw].rearrange("(k p) n -> p k n", p=128)
        self.dma("pool", slab[:, 0:kc, 0:gw], src, "slab%d" % j, w=[("slab", j)])
        return slab, ("slab", j)

    def linear(self, wl, K, c0, ncols, rhs_fn, nblk, evac, bw=512):
        KC = K // 128
        nparts = max(1, KC // 16)
        for g0 in range(c0, c0 + ncols, 256):
            gw = min(256, c0 + ncols - g0)
            nm = gw // 128
            banks = {}
            for part in range(nparts):
                kc = min(KC, 16)
                slab, skey = self.load_slab(wl, part * 2048, kc, g0, gw)
                for mi in range(nm):
                    for blk in range(nblk):
                        if part == 0:
                            banks[(mi, blk)] = self.next_bank()
                        b = banks[(mi, blk)]
                        rk = [skey]
                        mms = []
                        for k in range(kc):
                            rap, rkeys = rhs_fn(part * 16 + k, blk)
                            rk += rkeys
                            mms.append((slab[:, k, mi * 128:(mi + 1) * 128], rap,
                                        part == 0 and k == 0, part == nparts - 1 and k == kc - 1))
                        out = self.ps[b][:, 0:bw]

                        def fn(e, out=out, mms=mms):
                            ins = None
                            for lhsT, rhs, st, sp in mms:
                                ins = e.matmul(out, lhsT=lhsT, rhs=rhs, start=st, stop=sp)
                            return ins
                        self.op("pe", fn, r=rk, w=[("ps", b)])
                        if part == nparts - 1:
                            evac((g0 - c0) // 128 + mi, blk, self.ps[b][:, 0:bw], ("ps", b))

    def linear_swapped(self, wl, c0, ncols, lhs_fn, ntch, evac):
        for g0 in range(c0, c0 + ncols, 256):
            gw = min(256, c0 + ncols - g0)
            slab, skey = self.load_slab(wl, 0, 16, g0, gw)
            for tc in range(ntch):
                b = self.next_bank()
                rk = [skey]
                mms = []
                for k in range(16):
                    lap, lkeys = lhs_fn(k, tc)
                    rk += lkeys
                    mms.append((lap, slab[:, k, 0:gw], k == 0, k == 15))
                out = self.ps[b][:, 0:gw]

                def fn(e, out=out, mms=mms):
                    ins = None
                    for lhsT, rhs, st, sp in mms:
                        ins = e.matmul(out, lhsT=lhsT, rhs=rhs, start=st, stop=sp)
                    return ins
                self.op("pe", fn, r=rk, w=[("ps", b)])
                evac(tc, g0 - c0, gw, out, ("ps", b))

    def stats_rstd(self, sq_fn, nchunks, add_const, bw=512):
        b = self.misc_bank()
        rk = ["ones"]
        mms = []
        for c in range(nchunks):
            ap, keys = sq_fn(c)
            rk += keys
            mms.append((ap, c == 0, c == nchunks - 1))
        out = self.ps[b][:, 0:bw]
        ones = self.ones_bf

        def fn(e, out=out, mms=mms, ones=ones):
            ins = None
            for rhs, st, sp in mms:
                ins = e.matmul(out, lhsT=ones[:, :], rhs=rhs, start=st, stop=sp)
            return ins
        self.op("pe", fn, r=rk, w=[("ps", b)])
        ri = self.rst_rr % len(self.rst)
        self.rst_rr += 1
        rt, rkey = self.rst[ri], ("rst", ri)
        rs = rt[:, 0:bw]
        st_, skey_ = self.work()
        sq_ = st_[:, 0:bw]
        cb = self.c_dn[:, 0:1] if add_const > 1e-3 else self.c_hd[:, 0:1]
        self.op("act", lambda e: e.activation(out=sq_, in_=out, func=AF.Sqrt, bias=cb),
                r=[("ps", b), "cconst"], w=[skey_])
        self.op("dve", lambda e: e.reciprocal(out=rs, in_=sq_), r=[skey_], w=[rkey])
        return rs, rkey

    def prenorm(self, xv, blk, hv, hkey, gvec, svec, vkey):
        cs = slice(blk * 512, (blk + 1) * 512)
        hs = cs if hv.shape[2] > 512 else slice(0, 512)
        kb = blk if hv.shape[2] > 512 else 0
        for c in range(DC):
            xin = xv[:, c, cs]
            hout = hv[:, c, hs]
            self.op("act", lambda e, xin=xin, hout=hout: e.activation(out=hout, in_=xin, func=AF.Square),
                    r=[("x", c, blk)], w=[(hkey, c, kb)])
        rs, rkey = self.stats_rstd(lambda c: (hv[:, c, hs], [(hkey, c, kb)]), DC, D * EPS)
        for c in range(DC):
            xin = xv[:, c, cs]
            hout = hv[:, c, hs]
            t, tkey = self.work()
            self.op("dve", lambda e, xin=xin, t=t: e.tensor_tensor(out=t[:, :], in0=xin, in1=rs, op=ALU.mult),
                    r=[("x", c, blk), rkey], w=[tkey])
            g1 = gvec[:, c:c + 1]
            s1 = svec[:, c:c + 1]
            self.op("act", lambda e, t=t, hout=hout, g1=g1, s1=s1: e.activation(
                out=hout, in_=t[:, :], func=AF.Identity, bias=s1, scale=g1),
                r=[tkey, vkey], w=[(hkey, c, kb)])

    def postnorm_residual(self, xv, blk, yv, ykey, sqv, sqkey, gtvec, vkey):
        cs = slice(blk * 512, (blk + 1) * 512)
        for c in range(DC):
            yin = yv[:, c, :]
            so = sqv[:, c, :]
            self.op("act", lambda e, yin=yin, so=so: e.activation(out=so, in_=yin, func=AF.Square),
                    r=[(ykey, c)], w=[(sqkey, c, 0)])
        rs, rkey = self.stats_rstd(lambda c: (sqv[:, c, :], [(sqkey, c, 0)]), DC, D * EPS)
        for c in range(DC):
            yin = yv[:, c, :]
            xio = xv[:, c, cs]
            t, tkey = self.work()
            g1 = gtvec[:, c:c + 1]
            self.op("dve", lambda e, yin=yin, t=t, g1=g1: e.scalar_tensor_tensor(
                out=t[:, :], in0=yin, scalar=g1, in1=rs, op0=ALU.mult, op1=ALU.mult),
                r=[(ykey, c), rkey, vkey], w=[tkey])
            self.op("dve", lambda e, xio=xio, t=t: e.tensor_tensor(out=xio, in0=xio, in1=t[:, :], op=ALU.add),
                    r=[tkey, ("x", c, blk)], w=[("x", c, blk)])

    def attn_accumulate(self, nchunks, st_fn, pv_fn, scale, sum_b, o_banks, bw=512, st_pool=(0, 1, 2, 3)):
        pend = None
        for j in range(nchunks + 1):
            cur = None
            if j < nchunks:
                b = self.next_bank(st_pool)
                mm = st_fn(j)
                rk = []
                for m in mm:
                    rk += m[4]
                outb = self.ps[b]
                cnt = {}
                for m in mm:
                    cnt[(m[0], m[1])] = cnt.get((m[0], m[1]), 0) + 1
                seen = {}
                plan = []
                for m in mm:
                    key = (m[0], m[1])
                    i = seen.get(key, 0)
                    seen[key] = i + 1
                    plan.append((outb[:, m[0]:m[1]], m[2], m[3], i == 0, i == cnt[key] - 1))

                def fn(e, plan=plan):
                    ins = None
                    for o, lhsT, rhs, st, sp in plan:
                        ins = e.matmul(o, lhsT=lhsT, rhs=rhs, start=st, stop=sp)
                    return ins
                self.op("pe", fn, r=rk, w=[("ps", b)])
                et, ekey = self.etile()
                src = outb[:, 0:bw]
                dst = et[:, 0:bw]
                self.op("act", lambda e, src=src, dst=dst: e.activation(out=dst, in_=src, func=AF.Exp,
                                                                         scale=float(scale)),
                        r=[("ps", b)], w=[ekey])
                cur = (j, et, ekey)
            if pend is not None:
                pj, pet, pekey = pend
                so = self.ps[sum_b][:, 0:bw]
                ones = self.ones_bf
                rhs = pet[:, 0:bw]
                st, sp = pj == 0, pj == nchunks - 1
                self.op("pe", lambda e, so=so, rhs=rhs, st=st, sp=sp, ones=ones: e.matmul(
                    so, lhsT=ones[:, :], rhs=rhs, start=st, stop=sp), r=[pekey, "ones"], w=[("ps", sum_b)])
                pvs = pv_fn(pj)
                plan = []
                rk = [pekey]
                wk = set()
                for (oi, lo, hi, vl, keys, first, last) in pvs:
                    plan.append((self.ps[o_banks[oi]][:, lo:hi], vl, pet[:, lo:hi], first, last))
                    rk += keys
                    wk.add(("ps", o_banks[oi]))

                def fn2(e, plan=plan):
                    ins = None
                    for o, lhsT, rhs, st, sp in plan:
                        ins = e.matmul(o, lhsT=lhsT, rhs=rhs, start=st, stop=sp)
                    return ins
                self.op("pe", fn2, r=rk, w=list(wk))
            pend = cur

    def recip(self, bank, bw=512):
        t, tkey = self.work()
        src = self.ps[bank][:, 0:bw]
        self.op("dve", lambda e, t=t, src=src: e.reciprocal(out=t[:, 0:bw], in_=src), r=[("ps", bank)], w=[tkey])
        return t, tkey

    def build(self):
        nc = self.nc
        depth = self.depth
        B = self
        xsT = B.din("xsT", [D, 1024])
        xpT = B.din("xpT", [D, 512])
        cvec = B.din("cvec", [128, 32])
        cache = {}
        for nm, hw in (("nak", 512), ("nav", 512), ("gk", 256), ("gv", 256), ("dk", 512), ("dv", 512)):
            cache[nm] = B.din("c_" + nm, [L_FULL, 256, hw])
        w_ada = B.din("w_ada", [L_FULL, D, 6 * D])
        w_in = B.din("w_in", [L_FULL, D, D_IN])
        w_out = B.din("w_out", [L_FULL, D, D])
        w_up = B.din("w_up", [L_FULL, D, D_FF])
        w_down = B.din("w_down", [L_FULL, D_FF, D])
        badaT = B.din("badaT", [L_FULL, 128, 96])
        vecs = B.din("vecs", [128, L_FULL * 64 + 3 * L_FULL])
        lam_in = B.din("lam_in", [128, L_FULL * 256])
        tbin = B.din("tbin", [L_FULL, 128, 4 * TB_E * 64])
        ident_in = B.din("ident_in", [128, 128])
        perm_in = B.din("perm_in", [128, 256])
        vlr_in = B.din("vlr_in", [128, 2048])
        rope_in = B.din("rope_in", [2, 128, 2048])
        ysT = B.dout("ysT", [D, 1024])
        ypT = B.dout("ypT", [D, 512])
        okv = {}
        for nm, hw in (("nak", 512), ("nav", 512), ("gk", 256), ("gv", 256), ("dk", 512), ("dv", 512)):
            okv[nm] = B.dout("o_" + nm, [2, L_FULL, 256, hw])

        with ExitStack() as es:
            def sb(name, shape, dt):
                return es.enter_context(nc.sbuf_tensor(name, list(shape), dt))
            X = sb("X", [128, 16384], F32)
            RA = sb("RA", [128, 8192], F32)
            RB = sb("RB", [128, 16384], BF16)
            RC = sb("RC", [128, 16384], BF16)
            B.slabs = [sb("slab%d" % i, [128, 16, 256], BF16) for i in range(2)]
            B.wk = [sb("wk%d" % i, [128, 512], F32) for i in range(4)]
            B.rst = [sb("rst%d" % i, [128, 512], F32) for i in range(2)]
            B.et = [sb("et%d" % i, [128, 512], BF16) for i in range(2)]
            kcs = sb("kcs", [128, 2, 512], F32)
            ident = sb("ident", [128, 128], F32)
            B.ones_bf = sb("ones_bf", [128, 128], BF16)
            ident_bf = sb("ident_bf", [128, 128], BF16)
            perm = sb("perm", [128, 256], F32)
            vlr = sb("vlr", [128, 2048], BF16)
            cv = sb("cv", [128, 32], F32)
            cvs = sb("cvs", [128, 32], BF16)
            mt = sb("mt", [128, 192], F32)
            bada = sb("bada", [128, 96], F32)
            modS = sb("modS", [128, 96], F32)
            modP = sb("modP", [128, L_FULL * 96], F32)
            vec = sb("vec", [128, L_FULL * 64 + 3 * L_FULL], F32)
            vec2 = sb("vec2", [128, 8], F32)
            lamt = sb("lamt", [128, 256], F32)
            lamw = sb("lamw", [128, 136], F32)
            B.ps = [es.enter_context(nc.psum_tensor("ps%d" % i, [128, 512], F32)) for i in range(8)]

            RAb = RA.bitcast(BF16)

            B.dma("sp", ident[:, :], ident_in, "c0", w=["ident"])
            B.dma("sp", perm[:, :], perm_in, "c0", w=["perm"])
            B.dma("sp", cv[:, :], cvec, "c0", w=["cv"])
            B.dma("sp", vec[:, :], vecs, "c0", w=["vec"])
            B.dma("pool", vlr[:, :], vlr_in, "c1", w=["vlr"])
            B.op("dve", lambda e: e.memset(B.ones_bf[:, :], 1.0), w=["ones"])
            B.c_dn = sb("c_dn", [128, 1], F32)
            B.c_hd = sb("c_hd", [128, 1], F32)
            B.op("dve", lambda e: e.memset(B.c_dn[:, :], D * EPS), w=["cconst"])
            B.op("dve", lambda e: e.memset(B.c_hd[:, :], HD * EPS), w=["cconst"])
            B.op("dve", lambda e: e.tensor_copy(out=ident_bf[:, :], in_=ident[:, :]), r=["ident"], w=["identbf"])
            B.op("act", lambda e: e.activation(out=cvs[:, :], in_=cv[:, :], func=AF.Silu), r=["cv"], w=["cvs"])

            def phase(is_s):
                T = 1024 if is_s else 512
                NB = T // 512
                g = 0 if is_s else 1
                xv = X[:, 0:16 * T].rearrange("p (c t) -> p c t", c=16)
                xin = xsT if is_s else xpT
                xout = ysT if is_s else ypT
                hS = RAb[:, 0:16 * T].rearrange("p (c t) -> p c t", c=16)
                yblk = RA[:, 0:8192].rearrange("p (c t) -> p c t", c=16)
                QO = RB[:, 0:16 * T].rearrange("p (c t) -> p c t", c=16)
                hblk = RB[:, 0:8192].rearrange("p (c t) -> p c t", c=16)
                TK = T + (256 if is_s else 0)
                NCH = TK // 128
                kT = RC[:, 0:4 * TK].rearrange("p (h t) -> p h t", h=4)
                vv = RC[:, 5120:5120 + NCH * 512].rearrange("p (j d) -> p j d", j=NCH)
                tab = RC[:, 10240:10240 + 5632]
                tabf = RC.bitcast(F32)[:, 5120:5120 + 2816]
                sqA = RC[:, 0:8192].rearrange("p (c t) -> p c t", c=16)
                u2 = RC[:, 0:16384].rearrange("p (c t) -> p c t", c=32)
                pst = X[:, 8192:16384]
                kf32 = pst[:, 0:2048].rearrange("p (h t) -> p h t", h=4)
                ostg = [pst[:, 2048 + i * 512:2048 + (i + 1) * 512] for i in range(4)]

                B.S.barrier()
                for c4 in range(4):
                    B.dma("sp", xv[:, 4 * c4:4 * c4 + 4, :],
                          xin[512 * c4:512 * (c4 + 1), :].rearrange("(c p) t -> p c t", p=128), "xin",
                          w=[("x", c, blk) for c in range(4 * c4, 4 * c4 + 4) for blk in range(NB)],
                          post_barrier=True)

                B.stage('xload' + ('s' if is_s else 'p'))
                for l in range(depth):
                    lam_init = 0.8 - 0.6 * math.exp(-0.3 * l)
                    if is_s:
                        B.dma("sp", bada[:, :], badaT[l], "bada", w=["bada"])
                        mb = B.misc_bank()
                        mout = B.ps[mb][:, 0:192].rearrange("p (j v) -> p j v", v=2)
                        first = [True]

                        def ada_evac(mc, blk, bank_ap, bkey):
                            pass
                        for g0 in range(0, 6 * D, 256):
                            slab, skey = B.load_slab(w_ada[l], 0, 16, g0, 256)
                            plan = []
                            for mi in range(2):
                                j = g0 // 128 + mi
                                for k in range(16):
                                    plan.append((mout[:, j, :], slab[:, k, mi * 128:(mi + 1) * 128],
                                                 cvs[:, 2 * k:2 * k + 2], k == 0, k == 15))

                            def fn(e, plan=plan):
                                ins = None
                                for o, lhsT, rhs, st, sp in plan:
                                    ins = e.matmul(o, lhsT=lhsT, rhs=rhs, start=st, stop=sp)
                                return ins
                            B.op("pe", fn, r=[skey, "cvs"], w=[("ps", mb)])
                        mtv = mt[:, :].rearrange("p (j v) -> p j v", v=2)
                        for v_ in range(2):
                            B.op("dve", lambda e, v_=v_, mout=mout: e.tensor_tensor(out=mtv[:, :, v_], in0=mout[:, :, v_],
                                                                       in1=bada[:, :], op=ALU.add),
                                 r=[("ps", mb), "bada"], w=[("mt", v_)])
                        for v_, dst, dkey in ((0, modS[:, :], "modS"), (1, modP[:, l * 96:(l + 1) * 96], ("modP", l))):
                            def mchunk(n, v_=v_):
                                return mtv[:, n * 16:(n + 1) * 16, v_]

                            def ng(i):
                                return vec[:, l * 64 + i * 16: l * 64 + (i + 1) * 16]
                            steps = [
                                (dst[:, 0:16], mchunk(1), ng(0), True),
                                (dst[:, 16:32], mchunk(0), None, False),
                                (dst[:, 32:48], mchunk(2), ng(1), False),
                                (dst[:, 48:64], mchunk(4), ng(2), True),
                                (dst[:, 64:80], mchunk(3), None, False),
                                (dst[:, 80:96], mchunk(5), ng(3), False),
                            ]
                            for (o_, m_, g_, plus1) in steps:
                                if g_ is None:
                                    B.op("dve", lambda e, o_=o_, m_=m_: e.tensor_copy(out=o_, in_=m_),
                                         r=[("mt", v_)], w=[dkey])
                                else:
                                    B.op("dve", lambda e, o_=o_, m_=m_, g_=g_, plus1=plus1: e.scalar_tensor_tensor(
                                        out=o_, in0=m_, scalar=(1.0 if plus1 else 0.0), in1=g_,
                                        op0=ALU.add, op1=ALU.mult), r=[("mt", v_), "vec"], w=[dkey])
                                    B.op("dve", lambda e, o_=o_: e.tensor_scalar(out=o_, in0=o_, scalar1=SQD,
                                                                                scalar2=None, op0=ALU.mult),
                                         r=[dkey], w=[dkey])
                    B.stage('ada' + ('s' if is_s else 'p'))
                    mod = modS[:, :] if is_s else modP[:, l * 96:(l + 1) * 96]
                    mkey = "modS" if is_s else ("modP", l)
                    gA, shA, gtA = mod[:, 0:16], mod[:, 16:32], mod[:, 32:48]
                    gM, shM, gtM = mod[:, 48:64], mod[:, 64:80], mod[:, 80:96]

                    vb = L_FULL * 64
                    B.op("dve", lambda e, l=l: e.tensor_scalar(out=vec2[:, 0:1], in0=vec[:, vb + l:vb + l + 1],
                                                          scalar1=SQH, scalar2=None, op0=ALU.mult),
                         r=["vec"], w=["vec2"])
                    B.op("dve", lambda e, l=l: e.tensor_scalar(out=vec2[:, 1:2],
                                                          in0=vec[:, vb + L_FULL + l:vb + L_FULL + l + 1],
                                                          scalar1=SQH, scalar2=None, op0=ALU.mult),
                         r=["vec"], w=["vec2"])
                    B.op("dve", lambda e, l=l, lam_init=lam_init: e.tensor_scalar(out=vec2[:, 2:3],
                                                          in0=vec[:, vb + 2 * L_FULL + l:vb + 2 * L_FULL + l + 1],
                                                          scalar1=SQH * (1.0 - lam_init), scalar2=None, op0=ALU.mult),
                         r=["vec"], w=["vec2"])
                    B.dma("sp", lamt[:, :], lam_in[:, l * 256:(l + 1) * 256], "lamt", w=["lamt"])
                    B.op("dve", lambda e: e.tensor_tensor(out=lamw[:, 0:64], in0=lamt[:, 0:64], in1=lamt[:, 64:128],
                                                          op=ALU.mult), r=["lamt"], w=["lamw"])
                    B.op("dve", lambda e: e.tensor_tensor(out=lamw[:, 64:128], in0=lamt[:, 128:192],
                                                          in1=lamt[:, 192:256], op=ALU.mult), r=["lamt"], w=["lamw"])
                    B.op("dve", lambda e: e.reduce_sum(out=lamw[:, 128:130],
                                                       in_=lamw[:, 0:128].rearrange("p (a b) -> p a b", a=2),
                                                       axis=mybir.AxisListType.X), r=["lamw"], w=["lamw2"])
                    B.op("act", lambda e: e.activation(out=lamw[:, 130:132], in_=lamw[:, 128:130], func=AF.Exp),
                         r=["lamw2"], w=["lamw3"])
                    B.op("dve", lambda e: e.tensor_tensor(out=lamw[:, 132:133], in0=lamw[:, 131:132],
                                                          in1=lamw[:, 130:131], op=ALU.subtract),
                         r=["lamw3"], w=["lamw4"])
                    B.op("dve", lambda e, lam_init=lam_init: e.tensor_scalar(out=vec2[:, 3:4], in0=lamw[:, 132:133],
                                                          scalar1=-lam_init, scalar2=None, op0=ALU.add),
                         r=["lamw4"], w=["vec2"])
                    gq1, gk1, dg1, nlam = vec2[:, 0:1], vec2[:, 1:2], vec2[:, 2:3], vec2[:, 3:4]

                    B.S.barrier()
                    for blk in range(NB):
                        B.prenorm(xv, blk, hS, "h", gA, shA, mkey)
                    if l == 0:
                        B.tap("h0" + ("s" if is_s else "p"), hS, [("h", c, b_) for c in range(16) for b_ in range(NB)],
                              [128, 16, T])

                    B.stage('prenorm' + ('s' if is_s else 'p'))

                    def h_rhs(k, blk):
                        return hS[:, k, blk * 512:(blk + 1) * 512], [("h", k, blk)]

                    def h_lhs(k, tc):
                        return hS[:, k, tc * 128:(tc + 1) * 128], [("h", k, tc // 4)]

                    def load_ctx(knm, vnm, nh):
                        hw = nh * 128
                        B.dma("sp", kcs[:, :, 0:hw], cache[knm][l].rearrange("(j p) w -> p j w", p=128), "kcs",
                              w=["kcs"])
                        B.dma("pool", vv[:, 8:10, 0:hw], cache[vnm][l].rearrange("(j p) w -> p j w", p=128), "vctx",
                              w=[("v", 8), ("v", 9)], post_barrier=True)
                        for h0 in range(0, nh, 2):
                            mb = B.misc_bank()
                            outb = B.ps[mb]

                            def fn(e, h0=h0, outb=outb):
                                ins = None
                                for hh in range(2):
                                    for j in range(2):
                                        o = outb[:, (hh * 2 + j) * 128:(hh * 2 + j + 1) * 128]
                                        ins = e.transpose(o, kcs[:, j, (h0 + hh) * 128:(h0 + hh + 1) * 128],
                                                          ident[:, :])
                                return ins
                            B.op("pe", fn, r=["kcs", "ident"], w=[("ps", mb)])
                            dst = kT[:, h0:h0 + 2, T:T + 256]
                            src = outb[:, :].rearrange("p (h t) -> p h t", h=2)
                            B.copy_op(B.evac_eng(), dst, src, r=[("ps", mb)], w=[("kT", h0, 9), ("kT", h0 + 1, 9)])

                    def evac_plain(dst_fn, wkey_fn, scale=None, f32_fn=None):
                        def ev(mc, blk, bank_ap, bkey):
                            dst = dst_fn(mc, blk)
                            if f32_fn is not None:
                                d2, k2 = f32_fn(mc, blk)
                                B.op("dve", lambda e: e.tensor_copy(out=d2, in_=bank_ap), r=[bkey], w=k2)
                                B.op("act", lambda e: e.activation(out=dst, in_=d2, func=AF.Copy), r=k2,
                                     w=wkey_fn(mc, blk))
                                return
                            if scale is None:
                                B.copy_op(B.evac_eng(), dst, bank_ap, r=[bkey], w=wkey_fn(mc, blk))
                            else:
                                B.op("act", lambda e: e.activation(out=dst, in_=bank_ap, func=AF.Identity,
                                                                   scale=float(scale)), r=[bkey], w=wkey_fn(mc, blk))
                        return ev

                    def bs(blk):
                        return slice(blk * 512, (blk + 1) * 512)

                    def v_evac(nh, out_nm):
                        def ev(tc, coff, gw, bank_ap, bkey):
                            dst = vv[:, tc, coff:coff + gw]
                            if is_s:
                                B.copy_op(B.evac_eng(), dst, bank_ap, r=[bkey], w=[("v", tc)])
                            else:
                                i = B.ostg_rr % 4
                                B.ostg_rr += 1
                                st = ostg[i]
                                B.op("dve", lambda e: e.tensor_copy(out=st[:, 0:gw], in_=bank_ap),
                                     r=[bkey], w=[("ostg", i)])
                                B.op("act", lambda e: e.activation(out=dst, in_=st[:, 0:gw], func=AF.Copy),
                                     r=[("ostg", i)], w=[("v", tc)])
                                seq, tr = tc // 2, (tc % 2) * 128
                                B.dma("sp", okv[out_nm][seq, l, tr:tr + 128, coff:coff + gw], st[:, 0:gw],
                                      "ostg%d" % i, r=[("ostg", i)])
                        return ev

                    def k_out(nh, out_nm, src_f32, skey_fn):
                        for tc in range(4):
                            mb = B.misc_bank()
                            outb = B.ps[mb]

                            def fn(e, tc=tc, outb=outb):
                                ins = None
                                for h in range(nh):
                                    ins = e.transpose(outb[:, h * 128:(h + 1) * 128],
                                                      src_f32[:, h, tc * 128:(tc + 1) * 128], ident[:, :])
                                return ins
                            B.op("pe", fn, r=[skey_fn(h) for h in range(nh)] + ["ident"], w=[("ps", mb)])
                            i = B.ostg_rr % 4
                            B.ostg_rr += 1
                            st = ostg[i]
                            hw = nh * 128
                            B.copy_op(B.evac_eng(), st[:, 0:hw], outb[:, 0:hw], r=[("ps", mb)], w=[("ostg", i)])
                            seq, tr = tc // 2, (tc % 2) * 128
                            B.dma("sp", okv[out_nm][seq, l, tr:tr + 128, :], st[:, 0:hw], "ostg%d" % i,
                                  r=[("ostg", i)])

                    def mul_segs(dst, segs, R, rkey, wkeys):
                        for (bk, lo, hi) in segs:
                            B.op("dve", lambda e, bk=bk, lo=lo, hi=hi: e.tensor_tensor(
                                out=dst[:, lo:hi], in0=B.ps[bk][:, lo:hi], in1=R[:, lo:hi], op=ALU.mult),
                                r=[("ps", bk), rkey], w=wkeys)

                    def finish_head(sum_b, o_bs, slot, blk, bw=512):
                        R, rkey = B.recip(sum_b)
                        dst = QO[:, slot, bs(blk)]
                        segs = [(o_bs[0], 0, 512)] if len(o_bs) == 1 else [(o_bs[0], 0, 256), (o_bs[1], 256, 512)]
                        mul_segs(dst, segs, R, rkey, [("qo", slot, blk)])

                    acc_rr = [0]

                    def acc_pair():
                        a = acc_rr[0] % 2
                        acc_rr[0] += 1
                        return (4, 5) if a == 0 else (6, 7)

                    if is_s:
                        B.dma("pool", tab[:, :], tbin[l], "tab", w=["tab"], post_barrier=True)
                        load_ctx("nak", "nav", 4)
                    B.linear(w_in[l], D, C_NAQ, 512, h_rhs, NB,
                             evac_plain(lambda mc, blk: QO[:, mc, bs(blk)], lambda mc, blk: [("qo", mc, blk)],
                                        scale=SCALE_H))
                    B.linear(w_in[l], D, C_NAK, 512, h_rhs, NB,
                             evac_plain(lambda mc, blk: kT[:, mc, bs(blk)], lambda mc, blk: [("kT", mc, blk)],
                                        f32_fn=None if is_s else (lambda mc, blk: (kf32[:, mc, :], [("kf", mc)]))))
                    B.linear_swapped(w_in[l], C_NAV, 512, h_lhs, T // 128, v_evac(4, "nav"))
                    if not is_s:
                        k_out(4, "nak", kf32, lambda h: ("kf", h))
                    B.stage('na_proj' + ('s' if is_s else 'p'))
                    if is_s:
                        for h in range(4):
                            for blk in range(2):
                                chunks = list(range(0, 6)) if blk == 0 else list(range(2, 8))
                                chunks = chunks + [8, 9]
                                sum_b, o_b = acc_pair()

                                def st_fn(j, h=h, blk=blk, chunks=chunks):
                                    i = chunks[j]
                                    mm = [(0, 512, kT[:, h, i * 128:(i + 1) * 128], QO[:, h, bs(blk)],
                                           [("kT", h, i // 4 if i < 8 else 9), ("qo", h, blk)])]
                                    if i < 8:
                                        e0 = 10 - 2 * i + 8 * blk
                                        mm.append((0, 512, vlr[0:96, i * 128:(i + 1) * 128],
                                                   vlr[0:96, 1024 + blk * 512:1024 + (blk + 1) * 512], ["vlr"]))
                                        mm.append((0, 512, ident_bf[:, :],
                                                   tab[:, h * TB_E * 64 + e0 * 64:h * TB_E * 64 + (e0 + 8) * 64],
                                                   ["tab", "identbf"]))
                                    return mm

                                def pv_fn(j, h=h, chunks=chunks, o_b=o_b):
                                    i = chunks[j]
                                    return [(0, 0, 512, vv[:, i, h * 128:(h + 1) * 128], [("v", i)],
                                             j == 0, j == len(chunks) - 1)]
                                B.attn_accumulate(len(chunks), st_fn, pv_fn, 1.0, sum_b, [o_b])
                                finish_head(sum_b, [o_b], h, blk)
                    else:
                        for h in range(4):
                            sum_b, o_b, o_b2 = 4, 5, 6

                            def st_fn(j, h=h):
                                return [(0, 256, kT[:, h, j * 128:(j + 1) * 128], QO[:, h, 0:256],
                                         [("kT", h, 0), ("qo", h, 0)]),
                                        (256, 512, kT[:, h, 256 + j * 128:256 + (j + 1) * 128], QO[:, h, 256:512],
                                         [("kT", h, 0), ("qo", h, 0)])]

                            def pv_fn(j, h=h):
                                return [(0, 0, 256, vv[:, j, h * 128:(h + 1) * 128], [("v", j)], j == 0, j == 1),
                                        (1, 256, 512, vv[:, 2 + j, h * 128:(h + 1) * 128], [("v", 2 + j)],
                                         j == 0, j == 1)]
                            B.attn_accumulate(2, st_fn, pv_fn, 1.0, sum_b, [o_b, o_b2], st_pool=(0, 1, 2, 3))
                            finish_head(sum_b, [o_b, o_b2], h, 0)
                    if l == 0:
                        B.tap("ona" + ("s" if is_s else "p"), QO[:, 0:4, :],
                              [("qo", h, b_) for h in range(4) for b_ in range(NB)], [128, 4, T])

                    B.stage('na_attn' + ('s' if is_s else 'p'))
                    def normrope_evac(dst_fn, wkey_fn, gain, do_norm, rope_idx, f32_fn=None):
                        def ev(mc, blk, bank_ap, bkey):
                            dst = dst_fn(mc, blk)
                            wk_ = wkey_fn(mc, blk)
                            q32, qkey = B.work()
                            if do_norm:
                                et, ekey = B.etile()
                                B.op("dve", lambda e: e.tensor_copy(out=q32[:, :], in_=bank_ap), r=[bkey], w=[qkey])
                                B.op("act", lambda e: e.activation(out=et[:, :], in_=q32[:, :], func=AF.Square),
                                     r=[qkey], w=[ekey])
                                rs, rkey = B.stats_rstd(lambda c: (et[:, :], [ekey]), 1, HD * EPS)
                                need32 = (rope_idx is not None) or (f32_fn is not None)
                                tgt = q32[:, :] if need32 else dst
                                B.op("dve", lambda e: e.scalar_tensor_tensor(out=tgt, in0=q32[:, :], scalar=gain,
                                                                             in1=rs, op0=ALU.mult, op1=ALU.mult),
                                     r=[qkey, rkey, "vec2"], w=[qkey] if need32 else wk_)
                                if not need32:
                                    return
                            else:
                                B.copy_op("act", q32[:, :], bank_ap, r=[bkey], w=[qkey])
                            if f32_fn is not None:
                                d2, k2 = f32_fn(mc, blk)
                                B.op("act", lambda e: e.activation(out=d2, in_=q32[:, :], func=AF.Copy),
                                     r=[qkey], w=k2)
                            if rope_idx is None:
                                B.copy_op("dve", dst, q32[:, :], r=[qkey], w=wk_)
                                return
                            mb = B.misc_bank()
                            rb = B.ps[mb][:, 0:512]
                            pm = perm[:, rope_idx * 128:(rope_idx + 1) * 128]
                            B.op("pe", lambda e: e.matmul(rb, lhsT=pm, rhs=q32[:, :], start=True, stop=True),
                                 r=[qkey, "perm"], w=[("ps", mb)])
                            cosv = tabf[:, blk * 512:(blk + 1) * 512]
                            sinv = tabf[:, 1024 + blk * 512:1024 + (blk + 1) * 512]
                            t1, t1k = B.work()
                            t2, t2k = B.work()
                            B.op("dve", lambda e: e.tensor_tensor(out=t1[:, :], in0=q32[:, :], in1=cosv, op=ALU.mult),
                                 r=[qkey, "tab"], w=[t1k])
                            B.op("dve", lambda e: e.tensor_tensor(out=t2[:, :], in0=rb, in1=sinv, op=ALU.mult),
                                 r=[("ps", mb), "tab"], w=[t2k])
                            B.op("dve", lambda e: e.tensor_tensor(out=dst, in0=t1[:, :], in1=t2[:, :], op=ALU.add),
                                 r=[t1k, t2k], w=wk_)
                        return ev

                    if is_s:
                        B.dma("sp", tabf[:, 0:2048], rope_in[0], "tab", w=["tab"], r=[])
                        load_ctx("gk", "gv", 2)
                    B.linear(w_in[l], D, C_GQ, 1024, h_rhs, NB,
                             normrope_evac(lambda mc, blk: QO[:, 4 + mc, bs(blk)],
                                           lambda mc, blk: [("qo", 4 + mc, blk)], gq1, True, 0 if is_s else None))
                    B.linear(w_in[l], D, C_GK, 256, h_rhs, NB,
                             normrope_evac(lambda mc, blk: kT[:, mc, bs(blk)], lambda mc, blk: [("kT", mc, blk)],
                                           gk1, True, 0 if is_s else None,
                                           f32_fn=None if is_s else (lambda mc, blk: (kf32[:, mc, :], [("kf", mc)]))))
                    B.linear_swapped(w_in[l], C_GV, 256, h_lhs, T // 128, v_evac(2, "gv"))
                    if not is_s:
                        k_out(2, "gk", kf32, lambda h: ("kf", h))
                    B.stage('gqa_proj' + ('s' if is_s else 'p'))
                    for qh in range(8):
                        kvh = qh // 4
                        if is_s:
                            for blk in range(2):
                                sum_b, o_b = acc_pair()

                                def st_fn(j, qh=qh, kvh=kvh, blk=blk):
                                    return [(0, 512, kT[:, kvh, j * 128:(j + 1) * 128], QO[:, 4 + qh, bs(blk)],
                                             [("kT", kvh, j // 4 if j < 8 else 9), ("qo", 4 + qh, blk)])]

                                def pv_fn(j, kvh=kvh):
                                    return [(0, 0, 512, vv[:, j, kvh * 128:(kvh + 1) * 128], [("v", j)],
                                             j == 0, j == 9)]
                                B.attn_accumulate(10, st_fn, pv_fn, SCALE_H, sum_b, [o_b])
                                finish_head(sum_b, [o_b], 4 + qh, blk)
                        else:
                            sum_b, o_b, o_b2 = 4, 5, 6

                            def st_fn(j, qh=qh, kvh=kvh):
                                return [(0, 256, kT[:, kvh, j * 128:(j + 1) * 128], QO[:, 4 + qh, 0:256],
                                         [("kT", kvh, 0), ("qo", 4 + qh, 0)]),
                                        (256, 512, kT[:, kvh, 256 + j * 128:256 + (j + 1) * 128],
                                         QO[:, 4 + qh, 256:512], [("kT", kvh, 0), ("qo", 4 + qh, 0)])]

                            def pv_fn(j, kvh=kvh):
                                return [(0, 0, 256, vv[:, j, kvh * 128:(kvh + 1) * 128], [("v", j)], j == 0, j == 1),
                                        (1, 256, 512, vv[:, 2 + j, kvh * 128:(kvh + 1) * 128], [("v", 2 + j)],
                                         j == 0, j == 1)]
                            B.attn_accumulate(2, st_fn, pv_fn, SCALE_H, sum_b, [o_b, o_b2])
                            finish_head(sum_b, [o_b, o_b2], 4 + qh, 0)
                    if l == 0:
                        B.tap("ogq" + ("s" if is_s else "p"), QO[:, 4:12, :],
                              [("qo", h, b_) for h in range(4, 12) for b_ in range(NB)], [128, 8, T])

                    B.stage('gqa_attn' + ('s' if is_s else 'p'))
                    if is_s:
                        B.dma("sp", tabf[:, 0:2048], rope_in[1], "tab", w=["tab"])
                        load_ctx("dk", "dv", 4)
                    B.linear(w_in[l], D, C_DQ, 512, h_rhs, NB,
                             normrope_evac(lambda mc, blk: QO[:, 12 + mc, bs(blk)],
                                           lambda mc, blk: [("qo", 12 + mc, blk)], None, False, 1 if is_s else None))
                    B.linear(w_in[l], D, C_DK, 512, h_rhs, NB,
                             normrope_evac(lambda mc, blk: kT[:, mc, bs(blk)], lambda mc, blk: [("kT", mc, blk)],
                                           None, False, 1 if is_s else None,
                                           f32_fn=None if is_s else (lambda mc, blk: (kf32[:, mc, :], [("kf", mc)]))))
                    B.linear_swapped(w_in[l], C_DV, 512, h_lhs, T // 128, v_evac(4, "dv"))
                    if not is_s:
                        k_out(4, "dk", kf32, lambda h: ("kf", h))
                    B.stage('diff_proj' + ('s' if is_s else 'p'))
                    for h in range(4):
                        for blk in range(NB):
                            nchk = 10 if is_s else 2
                            for mp in range(2):
                                ps_ = slice(mp * 64, (mp + 1) * 64)
                                if is_s:
                                    sum_b, o_bs = ((4, [5]) if mp == 0 else (6, [7]))
                                else:
                                    sum_b, o_bs = ((2, [3, 4]) if mp == 0 else (5, [6, 7]))
                                if is_s:
                                    def st_fn(j, h=h, blk=blk, ps_=ps_):
                                        return [(0, 512, kT[ps_, h, j * 128:(j + 1) * 128], QO[ps_, 12 + h, bs(blk)],
                                                 [("kT", h, j // 4 if j < 8 else 9), ("qo", 12 + h, blk)])]

                                    def pv_fn(j, h=h):
                                        return [(0, 0, 512, vv[:, j, h * 128:(h + 1) * 128], [("v", j)],
                                                 j == 0, j == 9)]
                                else:
                                    def st_fn(j, h=h, ps_=ps_):
                                        return [(0, 256, kT[ps_, h, j * 128:(j + 1) * 128], QO[ps_, 12 + h, 0:256],
                                                 [("kT", h, 0), ("qo", 12 + h, 0)]),
                                                (256, 512, kT[ps_, h, 256 + j * 128:256 + (j + 1) * 128],
                                                 QO[ps_, 12 + h, 256:512], [("kT", h, 0), ("qo", 12 + h, 0)])]

                                    def pv_fn(j, h=h):
                                        return [(0, 0, 256, vv[:, j, h * 128:(h + 1) * 128], [("v", j)],
                                                 j == 0, j == 1),
                                                (1, 256, 512, vv[:, 2 + j, h * 128:(h + 1) * 128], [("v", 2 + j)],
                                                 j == 0, j == 1)]
                                B.attn_accumulate(nchk, st_fn, pv_fn, SCALE_DQ, sum_b, o_bs,
                                                  st_pool=(0, 1, 2, 3) if is_s else (0, 1))
                            R0, r0k = B.recip(4 if is_s else 2)
                            R1, r1k = B.recip(6 if is_s else 5)
                            t0, t0k = B.work()
                            t1, t1k = B.work()
                            if is_s:
                                segs0, segs1 = [(5, 0, 512)], [(7, 0, 512)]
                            else:
                                segs0, segs1 = [(3, 0, 256), (4, 256, 512)], [(6, 0, 256), (7, 256, 512)]
                            mul_segs(t0, segs0, R0, r0k, [t0k])
                            mul_segs(t1, segs1, R1, r1k, [t1k])
                            B.op("dve", lambda e, t0=t0, t1=t1: e.scalar_tensor_tensor(
                                out=t0[:, :], in0=t1[:, :], scalar=nlam, in1=t0[:, :], op0=ALU.mult, op1=ALU.add),
                                r=[t0k, t1k, "vec2"], w=[t0k])
                            et, ekey = B.etile()
                            B.op("act", lambda e, et=et, t0=t0: e.activation(out=et[:, :], in_=t0[:, :],
                                                                              func=AF.Square), r=[t0k], w=[ekey])
                            rs, rkey = B.stats_rstd(lambda c, et=et, ekey=ekey: (et[:, :], [ekey]), 1, HD * EPS)
                            dst = QO[:, 12 + h, bs(blk)]
                            B.op("dve", lambda e, t0=t0, rs=rs, dst=dst: e.scalar_tensor_tensor(
                                out=dst, in0=t0[:, :], scalar=dg1, in1=rs, op0=ALU.mult, op1=ALU.mult),
                                r=[t0k, rkey, "vec2"], w=[("qo", 12 + h, blk)])
                    if l == 0:
                        B.tap("odf" + ("s" if is_s else "p"), QO[:, 12:16, :],
                              [("qo", h, b_) for h in range(12, 16) for b_ in range(NB)], [128, 4, T])

                    B.stage('diff_attn' + ('s' if is_s else 'p'))
                    B.S.barrier()
                    for blk in range(NB):
                        def o_rhs(k, blk_, blk=blk):
                            return QO[:, k, bs(blk)], [("qo", k, blk)]

                        def y_evac(mc, blk_, bank_ap, bkey):
                            B.copy_op(B.evac_eng(), yblk[:, mc, :], bank_ap, r=[bkey], w=[("y", mc)])
                        B.linear(w_out[l], D, 0, D, o_rhs, 1, y_evac)
                        B.postnorm_residual(xv, blk, yblk, "y", sqA, "sqA", gtA, mkey)
                    if l == 0:
                        B.tap("x1" + ("s" if is_s else "p"), xv, [("x", c, b_) for c in range(16) for b_ in range(NB)],
                              [128, 16, T])

                    B.stage('wout' + ('s' if is_s else 'p'))
                    B.S.barrier()
                    for blk in range(NB):
                        B.prenorm(xv, blk, hblk, "hb", gM, shM, mkey)

                        def hb_rhs(k, blk_):
                            return hblk[:, k, :], [("hb", k, 0)]
                        for half in range(2):
                            def u_evac(mc, blk_, bank_ap, bkey):
                                dst = u2[:, mc, :]
                                tr_, trk_ = B.work()
                                B.op("dve", lambda e: e.tensor_scalar(
                                    out=tr_[:, :], in0=bank_ap, scalar1=0.0, scalar2=None, op0=ALU.max),
                                    r=[bkey], w=[trk_])
                                B.op("act", lambda e: e.activation(out=dst, in_=tr_[:, :], func=AF.Square),
                                     r=[trk_], w=[("u2", mc)])
                            B.linear(w_up[l], D, half * 4096, 4096, hb_rhs, 1, u_evac)

                            def u_rhs(k, blk_):
                                return u2[:, k, :], [("u2", k)]

                            def yd_evac(mc, blk_, bank_ap, bkey, half=half):
                                if half == 0:
                                    B.copy_op("act", yblk[:, mc, :], bank_ap, r=[bkey], w=[("y", mc)])
                                else:
                                    yo = yblk[:, mc, :]
                                    B.op("dve", lambda e: e.tensor_tensor(out=yo, in0=bank_ap, in1=yo, op=ALU.add),
                                         r=[bkey, ("y", mc)], w=[("y", mc)])
                            B.linear(w_down[l, half * 4096:(half + 1) * 4096, :], 4096, 0, D, u_rhs, 1, yd_evac)
                        B.postnorm_residual(xv, blk, yblk, "y", hblk, "hb", gtM, mkey)

                for c4 in range(4):
                    B.dma("sp", xout[512 * c4:512 * (c4 + 1), :].rearrange("(c p) t -> p c t", p=128),
                          xv[:, 4 * c4:4 * c4 + 4, :], "xout",
                          r=[("x", c, blk) for c in range(4 * c4, 4 * c4 + 4) for blk in range(NB)])

            try:
                phase(True)
                phase(False)
            except _Stop:
                pass

            with ExitStack() as es2:
                eng_sems = {k: es2.enter_context(nc.semaphore("sem_" + k)) for k in Sched.STREAMS}
                dma_sems = {k: es2.enter_context(nc.semaphore("dsem_" + str(i)))
                            for i, k in enumerate(sorted(self.dma_keys))}
                block = es2.enter_context(nc.Block())
                B.S.emit(nc, {"pe": block.tensor, "act": block.scalar, "dve": block.vector,
                              "pool": block.gpsimd, "sp": block.sync}, eng_sems, dma_sems)
        return nc


def _consts():
    ident = np.eye(128, dtype=np.float32)
    perm = np.zeros((128, 256), dtype=np.float32)
    for m in range(128):
        pg = m + 32 if (m % 64) < 32 else m - 32
        perm[pg, m] = 1.0
        pd = m + 16 if (m % 32) < 16 else m - 16
        perm[pd, 128 + m] = 1.0
    t = np.arange(1024)
    row = (t // GRID_W).astype(np.float32)
    col = (t % GRID_W).astype(np.float32)
    rope = np.zeros((2, 128, 2048), dtype=np.float32)
    for p in range(128):
        i = p % 32
        inv = np.float32(10000.0) ** (-np.float32(i) / np.float32(32))
        pos = row if p < 64 else col
        ang = (pos * inv).astype(np.float32)
        sgn = -1.0 if (p % 64) < 32 else 1.0
        rope[0, p, 0:1024] = np.cos(ang)
        rope[0, p, 1024:2048] = sgn * np.sin(ang)
        i = p % 16
        inv = np.float32(10000.0) ** (-np.float32(i) / np.float32(16))
        pos = row if (p % 64) < 32 else col
        ang = (pos * inv).astype(np.float32)
        sgn = -1.0 if (p % 32) < 16 else 1.0
        rope[1, p, 0:1024] = np.cos(ang)
        rope[1, p, 1024:2048] = sgn * np.sin(ang)
    vl = np.zeros((128, 1024), dtype=np.float32)
    vr = np.zeros((128, 1024), dtype=np.float32)
    s = np.arange(1024)
    rs, cs = s // 64, s % 64
    for k in range(16):
        vl[k, :] = (rs == k)
    for k in range(64):
        vl[16 + k, :] = (cs == k)
    rq, cq = rs, cs
    r0 = np.clip(rq - 4, 0, 8)
    ws = np.clip(cq - 8, 0, 48)
    for k in range(16):
        vr[k, :] = np.where((k >= r0) & (k < r0 + 8), 0.0, NEG)
    for k in range(64):
        vr[16 + k, :] = np.where((k >= ws) & (k < ws + 16), 0.0, NEG)
    vlr = np.concatenate([vl, vr], axis=1)
    return ident, perm, rope, vlr


def _tb_index():
    a = (np.arange(128) // 64)[:, None, None]
    cs = (np.arange(128) % 64)[:, None, None]
    e = np.arange(TB_E)[None, :, None]
    cq = np.arange(64)[None, None, :]
    d = a + 17 - e + 0 * cq
    dc = cs - cq + 15 + 0 * e
    ok = (d >= 0) & (d <= 14) & (dc >= 0) & (dc <= 30)
    return np.clip(d, 0, 14), np.clip(dc, 0, 30), ok


_CACHE = {}


def _get_program(depth, taps=(), stop=None):
    key = (depth, tuple(taps), stop)
    if key not in _CACHE:
        b = Builder(depth, taps, stop)
        nc = b.build()
        _CACHE[key] = (nc, b)
    return _CACHE[key]


def kernel(x_prompt, x_sample, c, cache_na_k, cache_na_v, cache_gqa_k, cache_gqa_v, cache_diff_k, cache_diff_v,
           c_ctx, w_ada, b_ada, norm_g, w_in, w_out, na_rpb, gqa_q_g, gqa_k_g, diff_lam, diff_g, w_up, w_down,
           _depth=L_FULL, _taps=(), _stop=None):
    f = lambda a: np.ascontiguousarray(np.asarray(a, dtype=np.float32))
    x_prompt, x_sample, c = f(x_prompt), f(x_sample), f(c)
    c_ctx = f(c_ctx)
    w_ada, w_in, w_out, w_up, w_down = f(w_ada), f(w_in), f(w_out), f(w_up), f(w_down)
    b_ada, norm_g, na_rpb = f(b_ada), f(norm_g), f(na_rpb)
    gqa_q_g, gqa_k_g, diff_lam, diff_g = f(gqa_q_g), f(gqa_k_g), f(diff_lam), f(diff_g)
    caches = {"nak": f(cache_na_k), "nav": f(cache_na_v), "gk": f(cache_gqa_k), "gv": f(cache_gqa_v),
              "dk": f(cache_diff_k), "dv": f(cache_diff_v)}
    nc, bld = _get_program(_depth, _taps, _stop)
    ident, perm, rope, vlr = _consts()
    didx, cidx, ok = _tb_index()
    L = L_FULL
    tb = np.zeros((L, 128, 4, TB_E, 64), dtype=np.float32)
    for h in range(4):
        gth = na_rpb[:, h][:, didx, cidx]
        tb[:, :, h] = np.where(ok[None], gth, np.float32(0.0))
    tb = np.ascontiguousarray(tb.reshape(L, 128, 4 * TB_E * 64))
    badaT = np.ascontiguousarray(b_ada.reshape(L, 96, 128).transpose(0, 2, 1))
    vecs = np.zeros((128, L * 64 + 3 * L), dtype=np.float32)
    vecs[:, 0:L * 64] = norm_g.reshape(L, 4, 16, 128).transpose(3, 0, 1, 2).reshape(128, L * 64)
    vecs[:, L * 64:L * 64 + L] = gqa_q_g.T
    vecs[:, L * 64 + L:L * 64 + 2 * L] = gqa_k_g.T
    vecs[:, L * 64 + 2 * L:L * 64 + 3 * L] = diff_g.T
    lam_in = np.ascontiguousarray(np.broadcast_to(diff_lam.reshape(1, L * 256), (128, L * 256)))
    in_maps = []
    for i in range(8):
        cvec = np.zeros((128, 16, 2), dtype=np.float32)
        cvec[:, :, 0] = c[i].reshape(16, 128).T
        cvec[:, :, 1] = c_ctx.reshape(16, 128).T
        xp = np.concatenate([x_prompt[2 * i], x_prompt[2 * i + 1]], axis=0)
        m = {
            "xsT": np.ascontiguousarray(x_sample[i].T), "xpT": np.ascontiguousarray(xp.T),
            "cvec": np.ascontiguousarray(cvec.reshape(128, 32)),
            "w_ada": w_ada, "w_in": w_in, "w_out": w_out, "w_up": w_up, "w_down": w_down,
            "badaT": badaT, "vecs": vecs, "lam_in": lam_in, "tbin": tb,
            "ident_in": ident, "perm_in": perm, "vlr_in": vlr, "rope_in": rope,
        }
        for nm, arr in caches.items():
            m["c_" + nm] = np.ascontiguousarray(arr[i].reshape(L, 256, -1))
        in_maps.append(m)
    res = run_bass_kernel_spmd(nc, in_maps, core_ids=list(range(8)))
    R = res.results
    y_prompt = np.zeros((16, 256, D), dtype=np.float32)
    y_sample = np.zeros((8, 1024, D), dtype=np.float32)
    outs = {nm: np.zeros((16, L, 256, hw), dtype=np.float32)
            for nm, hw in (("nak", 512), ("nav", 512), ("gk", 256), ("gv", 256), ("dk", 512), ("dv", 512))}
    for i in range(8):
        y_sample[i] = R[i]["ysT"].T
        yp = R[i]["ypT"].T
        y_prompt[2 * i] = yp[0:256]
        y_prompt[2 * i + 1] = yp[256:512]
        for nm in outs:
            outs[nm][2 * i:2 * i + 2] = R[i]["o_" + nm]
    if _taps:
        kernel.last_taps = [{k: R[i]["tap_" + k] for k in bld.tap_outs} for i in range(8)]
    return (y_prompt, y_sample,
            outs["nak"].reshape(16, L, 256, 4, 128), outs["nav"].reshape(16, L, 256, 4, 128),
            outs["gk"].reshape(16, L, 256, 2, 128), outs["gv"].reshape(16, L, 256, 2, 128),
            outs["dk"].reshape(16, L, 256, 4, 128), outs["dv"].reshape(16, L, 256, 4, 128))
```

```python
import math
from contextlib import ExitStack

import numpy as np
import concourse.bass as bass
import concourse.mybir as mybir
from concourse.bass_utils import run_bass_kernel_spmd

F32 = mybir.dt.float32
BF16 = mybir.dt.bfloat16
AF = mybir.ActivationFunctionType
ALU = mybir.AluOpType

D = 2048
DC = 16
L_FULL = 4
HD = 128
EPS = 1e-6
D_IN = 4608
D_FF = 8192
GRID_W = 64
NEG = -30000.0
SQD = math.sqrt(2048.0)
SQH = math.sqrt(128.0)
SCALE_H = 128.0 ** -0.5
SCALE_DQ = 64.0 ** -0.5
TB_E = 22

C_NAQ, C_NAK, C_NAV = 0, 512, 1024
C_GQ, C_GK, C_GV = 1536, 2560, 2816
C_DQ, C_DK, C_DV = 3072, 3584, 4096


class Op:
    __slots__ = ("eng", "fn", "dma", "dma_val", "signal", "deps", "cnt")


class Sched:
    STREAMS = ("pe", "act", "dve", "pool", "sp")

    def __init__(self):
        self.streams = {k: [] for k in self.STREAMS}
        self.res = {}
        self.dma_cum = {}
        self.bar_deps = {k: [] for k in self.STREAMS}
        self.last_barrier = []
        self.sp_latest = {}
        self.nops = 0

    def add(self, eng, fn, r=(), w=(), dma=None, post_barrier=False):
        op = Op()
        op.eng, op.fn, op.dma, op.signal, op.deps, op.cnt = eng, fn, dma, False, [], 0
        op.dma_val = 0
        cand = []
        for key in r:
            st = self.res.get(key)
            if st is not None and st[0] is not None:
                cand.append((st[0], "raw"))
        for key in w:
            st = self.res.get(key)
            if st is not None:
                if st[0] is not None:
                    cand.append((st[0], "waw"))
                for rd in st[1]:
                    cand.append((rd, "war"))
        for p in self.bar_deps[eng]:
            cand.append((p, "bar"))
        self.bar_deps[eng] = []
        if post_barrier:
            for p in self.last_barrier:
                cand.append((p, "bar"))
        seen = set()
        for p, kind in cand:
            if id(p) in seen:
                continue
            if p.dma is None and p.eng == eng:
                if eng == "pe" and dma is None:
                    continue
            seen.add(id(p))
            if p.dma is not None:
                op.deps.append((p, self.dma_cum[p.dma]))
            else:
                p.signal = True
                op.deps.append((p, None))
        for key in r:
            st = self.res.setdefault(key, [None, []])
            st[1].append(op)
        for key in w:
            self.res[key] = [op, []]
        if dma is not None:
            self.dma_cum[dma] = self.dma_cum.get(dma, 0) + 16
            op.dma_val = self.dma_cum[dma]
            if eng == "sp":
                self.sp_latest[dma] = op
        self.streams[eng].append(op)
        self.nops += 1
        return op

    def barrier(self, engs=("pe", "act", "dve", "sp")):
        lasts = []
        for e in engs:
            if self.streams[e]:
                lasts.append(self.streams[e][-1])
        for k_, p_ in self.sp_latest.items():
            if p_ not in lasts:
                lasts.append(p_)
        self.sp_latest = {}
        for e in engs:
            self.bar_deps[e] = [p for p in lasts if p.eng != e or p.dma is not None]
        self.last_barrier = lasts

    def emit(self, nc, block_engines, eng_sems, dma_sems):
        for name in self.STREAMS:
            cnt = 0
            for op in self.streams[name]:
                if op.dma is None and op.signal:
                    cnt += 1
                op.cnt = cnt

        def run_stream(name):
            def body(e):
                waited = {}
                for op in self.streams[name]:
                    for p, v in op.deps:
                        if p.dma is not None:
                            sem, val, sid = dma_sems[p.dma], v, ("d", p.dma)
                        else:
                            sem, val, sid = eng_sems[p.eng], p.cnt, ("e", p.eng)
                        if waited.get(sid, 0) >= val:
                            continue
                        e.wait_ge(sem, val)
                        waited[sid] = val
                    ins = op.fn(e)
                    if op.dma is not None:
                        ins.then_inc(dma_sems[op.dma], 16)
                    elif op.signal:
                        ins.then_inc(eng_sems[name], 1)
                if name == "sp":
                    for key, val in self.dma_cum.items():
                        if waited.get(("d", key), 0) < val:
                            e.wait_ge(dma_sems[key], val)
            return body

        block_engines["pe"](run_stream("pe"))
        block_engines["act"](run_stream("act"))
        block_engines["dve"](run_stream("dve"))
        block_engines["pool"](run_stream("pool"))
        block_engines["sp"](run_stream("sp"))


class _Stop(Exception):
    pass


class Builder:
    def __init__(self, depth=L_FULL, taps=(), stop=None):
        self.stop = stop
        self.depth = depth
        self.taps = set(taps)
        self.tap_outs = {}
        self.nc = bass.Bass("TRN2", target_bir_lowering=False)
        self.S = Sched()
        self.dma_keys = set()
        self.bank_rr = 0
        self.misc_rr = 0
        self.slab_rr = 0
        self.evac_rr = 0
        self.work_rr = 0
        self.e_rr = 0
        self.rst_rr = 0
        self.slab_wide = False
        self.ostg_rr = 0

    def din(self, name, shape):
        return self.nc.dram_tensor(name, list(shape), F32, kind="ExternalInput").ap()

    def dout(self, name, shape):
        return self.nc.dram_tensor(name, list(shape), F32, kind="ExternalOutput").ap()

    def op(self, eng, fn, r=(), w=(), **kw):
        return self.S.add(eng, fn, r=r, w=w, **kw)

    def dma(self, eng, out, in_, key, r=(), w=(), post_barrier=False):
        key = eng + ":" + key
        self.dma_keys.add(key)
        return self.S.add(eng, lambda e, out=out, in_=in_: e.dma_start(out=out, in_=in_), r=r, w=w, dma=key,
                          post_barrier=post_barrier)

    def next_bank(self, pool=(0, 1, 2, 3, 4, 5)):
        b = pool[self.bank_rr % len(pool)]
        self.bank_rr += 1
        return b

    def misc_bank(self):
        b = 6 + (self.misc_rr % 2)
        self.misc_rr += 1
        return b

    def work(self):
        i = self.work_rr % len(self.wk)
        self.work_rr += 1
        return self.wk[i], ("wk", i)

    def etile(self):
        i = self.e_rr % len(self.et)
        self.e_rr += 1
        return self.et[i], ("et", i)

    def evac_eng(self):
        self.evac_rr += 1
        return "act" if self.evac_rr % 2 else "dve"

    def copy_op(self, eng, out, in_, r, w):
        if eng == "act":
            return self.op("act", lambda e: e.activation(out=out, in_=in_, func=AF.Copy), r=r, w=w)
        return self.op("dve", lambda e: e.tensor_copy(out=out, in_=in_), r=r, w=w)

    def stage(self, name):
        if self.stop == name:
            raise _Stop()

    def tap(self, name, ap, rkeys, shape):
        if name not in self.taps:
            return
        o = self.nc.dram_tensor("tap_" + name, list(shape), ap.dtype, kind="ExternalOutput").ap()
        self.tap_outs[name] = (shape, ap.dtype)
        self.dma("sp", o, ap, "tap_" + name, r=rkeys)

    def load_slab(self, wl, k0, kc, c0, gw):
        npool = 4 if self.slab_wide else 2
        j = self.slab_rr % npool
        self.slab_rr += 1
        slab = self.slabs[j]
        src = wl[k0:k0 + kc * 128, c0:c0 + gw].rearrange("(k p) n -> p k n", p=128)
        self.dma("pool", slab[:, 0:kc, 0:gw], src, "slab%d" % j, w=[("slab", j)], post_barrier=(j >= 2))
        return slab, ("slab", j)

    def linear(self, wl, K, c0, ncols, rhs_fn, nblk, evac, bw=512):
        KC = K // 128
        nparts = max(1, KC // 16)
        for g0 in range(c0, c0 + ncols, 256):
            gw = min(256, c0 + ncols - g0)
            nm = gw // 128
            banks = {}
            for part in range(nparts):
                kc = min(KC, 16)
                slab, skey = self.load_slab(wl, part * 2048, kc, g0, gw)
                for mi in range(nm):
                    for blk in range(nblk):
                        if part == 0:
                            banks[(mi, blk)] = self.next_bank()
                        b = banks[(mi, blk)]
                        rk = [skey]
                        mms = []
                        for k in range(kc):
                            rap, rkeys = rhs_fn(part * 16 + k, blk)
                            rk += rkeys
                            mms.append((slab[:, k, mi * 128:(mi + 1) * 128], rap,
                                        part == 0 and k == 0, part == nparts - 1 and k == kc - 1))
                        out = self.ps[b][:, 0:bw]

                        def fn(e, out=out, mms=mms):
                            ins = None
                            for lhsT, rhs, st, sp in mms:
                                ins = e.matmul(out, lhsT=lhsT, rhs=rhs, start=st, stop=sp)
                            return ins
                        self.op("pe", fn, r=rk, w=[("ps", b)])
                        if part == nparts - 1:
                            evac((g0 - c0) // 128 + mi, blk, self.ps[b][:, 0:bw], ("ps", b))

    def linear_swapped(self, wl, c0, ncols, lhs_fn, ntch, evac):
        for g0 in range(c0, c0 + ncols, 256):
            gw = min(256, c0 + ncols - g0)
            slab, skey = self.load_slab(wl, 0, 16, g0, gw)
            for tc in range(ntch):
                b = self.next_bank()
                rk = [skey]
                mms = []
                for k in range(16):
                    lap, lkeys = lhs_fn(k, tc)
                    rk += lkeys
                    mms.append((lap, slab[:, k, 0:gw], k == 0, k == 15))
                out = self.ps[b][:, 0:gw]

                def fn(e, out=out, mms=mms):
                    ins = None
                    for lhsT, rhs, st, sp in mms:
                        ins = e.matmul(out, lhsT=lhsT, rhs=rhs, start=st, stop=sp)
                    return ins
                self.op("pe", fn, r=rk, w=[("ps", b)])
                evac(tc, g0 - c0, gw, out, ("ps", b))

    def stats_rstd(self, sq_fn, nchunks, add_const, bw=512):
        b = self.misc_bank()
        rk = ["ones"]
        mms = []
        for c in range(nchunks):
            ap, keys = sq_fn(c)
            rk += keys
            mms.append((ap, c == 0, c == nchunks - 1))
        out = self.ps[b][:, 0:bw]
        ones = self.ones_bf

        def fn(e, out=out, mms=mms, ones=ones):
            ins = None
            for rhs, st, sp in mms:
                ins = e.matmul(out, lhsT=ones[:, :], rhs=rhs, start=st, stop=sp)
            return ins
        self.op("pe", fn, r=rk, w=[("ps", b)])
        ri = self.rst_rr % len(self.rst)
        self.rst_rr += 1
        rt, rkey = self.rst[ri], ("rst", ri)
        rs = rt[:, 0:bw]
        st_, skey_ = self.work()
        sq_ = st_[:, 0:bw]
        cb = self.c_dn[:, 0:1] if add_const > 1e-3 else self.c_hd[:, 0:1]
        self.op("act", lambda e: e.activation(out=sq_, in_=out, func=AF.Sqrt, bias=cb),
                r=[("ps", b), "cconst"], w=[skey_])
        self.op("dve", lambda e: e.reciprocal(out=rs, in_=sq_), r=[skey_], w=[rkey])
        return rs, rkey

    def prenorm(self, xv, blk, hv, hkey, gvec, svec, vkey):
        cs = slice(blk * 512, (blk + 1) * 512)
        hs = cs if hv.shape[2] > 512 else slice(0, 512)
        kb = blk if hv.shape[2] > 512 else 0
        for c in range(DC):
            xin = xv[:, c, cs]
            hout = hv[:, c, hs]
            self.op("act", lambda e, xin=xin, hout=hout: e.activation(out=hout, in_=xin, func=AF.Square),
                    r=[("x", c, blk)], w=[(hkey, c, kb)])
        rs, rkey = self.stats_rstd(lambda c: (hv[:, c, hs], [(hkey, c, kb)]), DC, D * EPS)
        for c in range(DC):
            xin = xv[:, c, cs]
            hout = hv[:, c, hs]
            t, tkey = self.work()
            self.op("dve", lambda e, xin=xin, t=t: e.tensor_tensor(out=t[:, :], in0=xin, in1=rs, op=ALU.mult),
                    r=[("x", c, blk), rkey], w=[tkey])
            g1 = gvec[:, c:c + 1]
            s1 = svec[:, c:c + 1]
            self.op("act", lambda e, t=t, hout=hout, g1=g1, s1=s1: e.activation(
                out=hout, in_=t[:, :], func=AF.Identity, bias=s1, scale=g1),
                r=[tkey, vkey], w=[(hkey, c, kb)])

    def postnorm_residual(self, xv, blk, yv, ykey, sqv, sqkey, gtvec, vkey):
        cs = slice(blk * 512, (blk + 1) * 512)
        for c in range(DC):
            yin = yv[:, c, :]
            so = sqv[:, c, :]
            self.op("act", lambda e, yin=yin, so=so: e.activation(out=so, in_=yin, func=AF.Square),
                    r=[(ykey, c)], w=[(sqkey, c, 0)])
        rs, rkey = self.stats_rstd(lambda c: (sqv[:, c, :], [(sqkey, c, 0)]), DC, D * EPS)
        for c in range(DC):
            yin = yv[:, c, :]
            xio = xv[:, c, cs]
            t, tkey = self.work()
            g1 = gtvec[:, c:c + 1]
            self.op("dve", lambda e, yin=yin, t=t, g1=g1: e.scalar_tensor_tensor(
                out=t[:, :], in0=yin, scalar=g1, in1=rs, op0=ALU.mult, op1=ALU.mult),
                r=[(ykey, c), rkey, vkey], w=[tkey])
            self.op("dve", lambda e, xio=xio, t=t: e.tensor_tensor(out=xio, in0=xio, in1=t[:, :], op=ALU.add),
                    r=[tkey, ("x", c, blk)], w=[("x", c, blk)])

    def attn_accumulate(self, nchunks, st_fn, pv_fn, scale, sum_b, o_banks, bw=512, st_pool=(0, 1, 2, 3)):
        pend = None
        for j in range(nchunks + 1):
            cur = None
            if j < nchunks:
                b = self.next_bank(st_pool)
                mm = st_fn(j)
                rk = []
                for m in mm:
                    rk += m[4]
                outb = self.ps[b]
                cnt = {}
                for m in mm:
                    cnt[(m[0], m[1])] = cnt.get((m[0], m[1]), 0) + 1
                seen = {}
                plan = []
                for m in mm:
                    key = (m[0], m[1])
                    i = seen.get(key, 0)
                    seen[key] = i + 1
                    plan.append((outb[:, m[0]:m[1]], m[2], m[3], i == 0, i == cnt[key] - 1))

                def fn(e, plan=plan):
                    ins = None
                    for o, lhsT, rhs, st, sp in plan:
                        ins = e.matmul(o, lhsT=lhsT, rhs=rhs, start=st, stop=sp)
                    return ins
                self.op("pe", fn, r=rk, w=[("ps", b)])
                et, ekey = self.etile()
                src = outb[:, 0:bw]
                dst = et[:, 0:bw]
                self.op("act", lambda e, src=src, dst=dst: e.activation(out=dst, in_=src, func=AF.Exp,
                                                                         scale=float(scale)),
                        r=[("ps", b)], w=[ekey])
                cur = (j, et, ekey)
            if pend is not None:
                pj, pet, pekey = pend
                so = self.ps[sum_b][:, 0:bw]
                ones = self.ones_bf
                rhs = pet[:, 0:bw]
                st, sp = pj == 0, pj == nchunks - 1
                self.op("pe", lambda e, so=so, rhs=rhs, st=st, sp=sp, ones=ones: e.matmul(
                    so, lhsT=ones[:, :], rhs=rhs, start=st, stop=sp), r=[pekey, "ones"], w=[("ps", sum_b)])
                pvs = pv_fn(pj)
                plan = []
                rk = [pekey]
                wk = set()
                for (oi, lo, hi, vl, keys, first, last) in pvs:
                    plan.append((self.ps[o_banks[oi]][:, lo:hi], vl, pet[:, lo:hi], first, last))
                    rk += keys
                    wk.add(("ps", o_banks[oi]))

                def fn2(e, plan=plan):
                    ins = None
                    for o, lhsT, rhs, st, sp in plan:
                        ins = e.matmul(o, lhsT=lhsT, rhs=rhs, start=st, stop=sp)
                    return ins
                self.op("pe", fn2, r=rk, w=list(wk))
            pend = cur

    def recip(self, bank, bw=512):
        t, tkey = self.work()
        src = self.ps[bank][:, 0:bw]
        self.op("dve", lambda e, t=t, src=src: e.reciprocal(out=t[:, 0:bw], in_=src), r=[("ps", bank)], w=[tkey])
        return t, tkey

    def build(self):
        nc = self.nc
        depth = self.depth
        B = self
        xsT = B.din("xsT", [D, 1024])
        xpT = B.din("xpT", [D, 512])
        cvec = B.din("cvec", [128, 32])
        cache = {}
        for nm, hw in (("nak", 512), ("nav", 512), ("gk", 256), ("gv", 256), ("dk", 512), ("dv", 512)):
            cache[nm] = B.din("c_" + nm, [L_FULL, 256, hw])
        w_ada = B.din("w_ada", [L_FULL, D, 6 * D])
        w_in = B.din("w_in", [L_FULL, D, D_IN])
        w_out = B.din("w_out", [L_FULL, D, D])
        w_up = B.din("w_up", [L_FULL, D, D_FF])
        w_down = B.din("w_down", [L_FULL, D_FF, D])
        badaT = B.din("badaT", [L_FULL, 128, 96])
        vecs = B.din("vecs", [128, L_FULL * 64 + 3 * L_FULL])
        lam_in = B.din("lam_in", [128, L_FULL * 256])
        tbin = B.din("tbin", [L_FULL, 128, 4 * TB_E * 64])
        ident_in = B.din("ident_in", [128, 128])
        perm_in = B.din("perm_in", [128, 256])
        vlr_in = B.din("vlr_in", [128, 2048])
        rope_in = B.din("rope_in", [2, 128, 2048])
        ysT = B.dout("ysT", [D, 1024])
        ypT = B.dout("ypT", [D, 512])
        okv = {}
        for nm, hw in (("nak", 512), ("nav", 512), ("gk", 256), ("gv", 256), ("dk", 512), ("dv", 512)):
            okv[nm] = B.dout("o_" + nm, [2, L_FULL, 256, hw])

        with ExitStack() as es:
            def sb(name, shape, dt):
                return es.enter_context(nc.sbuf_tensor(name, list(shape), dt))
            X = sb("X", [128, 16384], F32)
            RA = sb("RA", [128, 8192], F32)
            RB = sb("RB", [128, 16384], BF16)
            RC = sb("RC", [128, 16384], BF16)
            B.slabs = [sb("slab%d" % i, [128, 16, 256], BF16) for i in range(2)]
            B.wk = [sb("wk%d" % i, [128, 512], F32) for i in range(4)]
            B.rst = [sb("rst%d" % i, [128, 512], F32) for i in range(2)]
            B.et = [sb("et%d" % i, [128, 512], BF16) for i in range(2)]
            kcs = sb("kcs", [128, 2, 512], F32)
            ident = sb("ident", [128, 128], F32)
            B.ones_bf = sb("ones_bf", [128, 128], BF16)
            ident_bf = sb("ident_bf", [128, 128], BF16)
            perm = sb("perm", [128, 256], F32)
            vlr = sb("vlr", [128, 2048], BF16)
            cv = sb("cv", [128, 32], F32)
            cvs = sb("cvs", [128, 32], BF16)
            mt = sb("mt", [128, 192], F32)
            bada = sb("bada", [128, 96], F32)
            modS = sb("modS", [128, 96], F32)
            modP = sb("modP", [128, L_FULL * 96], F32)
            vec = sb("vec", [128, L_FULL * 64 + 3 * L_FULL], F32)
            vec2 = sb("vec2", [128, 8], F32)
            lamt = sb("lamt", [128, 256], F32)
            lamw = sb("lamw", [128, 136], F32)
            B.ps = [es.enter_context(nc.psum_tensor("ps%d" % i, [128, 512], F32)) for i in range(8)]

            RAb = RA.bitcast(BF16)
            B.slabs = B.slabs + [RB[:, 8192 + i * 4096:8192 + (i + 1) * 4096].rearrange("p (k n) -> p k n", k=16)
                                 for i in range(2)]

            B.dma("sp", ident[:, :], ident_in, "c0", w=["ident"])
            B.dma("sp", perm[:, :], perm_in, "c0", w=["perm"])
            B.dma("sp", cv[:, :], cvec, "c0", w=["cv"])
            B.dma("sp", vec[:, :], vecs, "c0", w=["vec"])
            B.dma("pool", vlr[:, :], vlr_in, "c1", w=["vlr"])
            B.op("dve", lambda e: e.memset(B.ones_bf[:, :], 1.0), w=["ones"])
            B.c_dn = sb("c_dn", [128, 1], F32)
            B.c_hd = sb("c_hd", [128, 1], F32)
            B.op("dve", lambda e: e.memset(B.c_dn[:, :], D * EPS), w=["cconst"])
            B.op("dve", lambda e: e.memset(B.c_hd[:, :], HD * EPS), w=["cconst"])
            B.op("dve", lambda e: e.tensor_copy(out=ident_bf[:, :], in_=ident[:, :]), r=["ident"], w=["identbf"])
            B.op("act", lambda e: e.activation(out=cvs[:, :], in_=cv[:, :], func=AF.Silu), r=["cv"], w=["cvs"])

            def phase(is_s):
                T = 1024 if is_s else 512
                NB = T // 512
                g = 0 if is_s else 1
                xv = X[:, 0:16 * T].rearrange("p (c t) -> p c t", c=16)
                xin = xsT if is_s else xpT
                xout = ysT if is_s else ypT
                hS = RAb[:, 0:16 * T].rearrange("p (c t) -> p c t", c=16)
                yblk = RA[:, 0:8192].rearrange("p (c t) -> p c t", c=16)
                QO = RB[:, 0:16 * T].rearrange("p (c t) -> p c t", c=16)
                hblk = RB[:, 0:8192].rearrange("p (c t) -> p c t", c=16)
                TK = T + (256 if is_s else 0)
                NCH = TK // 128
                kT = RC[:, 0:4 * TK].rearrange("p (h t) -> p h t", h=4)
                vv = RC[:, 5120:5120 + NCH * 512].rearrange("p (j d) -> p j d", j=NCH)
                tab = RC[:, 10240:10240 + 5632]
                tabf = RC.bitcast(F32)[:, 5120:5120 + 2816]
                sqA = RC[:, 0:8192].rearrange("p (c t) -> p c t", c=16)
                u2 = RC[:, 0:16384].rearrange("p (c t) -> p c t", c=32)
                pst = X[:, 8192:16384]
                kf32 = pst[:, 0:2048].rearrange("p (h t) -> p h t", h=4)
                ostg = [pst[:, 2048 + i * 512:2048 + (i + 1) * 512] for i in range(4)]

                B.S.barrier()
                for c4 in range(4):
                    B.dma("sp", xv[:, 4 * c4:4 * c4 + 4, :],
                          xin[512 * c4:512 * (c4 + 1), :].rearrange("(c p) t -> p c t", p=128), "xin",
                          w=[("x", c, blk) for c in range(4 * c4, 4 * c4 + 4) for blk in range(NB)],
                          post_barrier=True)

                B.stage('xload' + ('s' if is_s else 'p'))
                for l in range(depth):
                    lam_init = 0.8 - 0.6 * math.exp(-0.3 * l)
                    if is_s:
                        B.slab_wide = True
                        B.dma("sp", bada[:, :], badaT[l], "bada", w=["bada"])
                        mb = B.misc_bank()
                        mout = B.ps[mb][:, 0:192].rearrange("p (j v) -> p j v", v=2)
                        first = [True]

                        def ada_evac(mc, blk, bank_ap, bkey):
                            pass
                        for g0 in range(0, 6 * D, 256):
                            slab, skey = B.load_slab(w_ada[l], 0, 16, g0, 256)
                            plan = []
                            for mi in range(2):
                                j = g0 // 128 + mi
                                for k in range(16):
                                    plan.append((mout[:, j, :], slab[:, k, mi * 128:(mi + 1) * 128],
                                                 cvs[:, 2 * k:2 * k + 2], k == 0, k == 15))

                            def fn(e, plan=plan):
                                ins = None
                                for o, lhsT, rhs, st, sp in plan:
                                    ins = e.matmul(o, lhsT=lhsT, rhs=rhs, start=st, stop=sp)
                                return ins
                            B.op("pe", fn, r=[skey, "cvs"], w=[("ps", mb)])
                        mtv = mt[:, :].rearrange("p (j v) -> p j v", v=2)
                        for v_ in range(2):
                            B.op("dve", lambda e, v_=v_, mout=mout: e.tensor_tensor(out=mtv[:, :, v_], in0=mout[:, :, v_],
                                                                       in1=bada[:, :], op=ALU.add),
                                 r=[("ps", mb), "bada"], w=[("mt", v_)])
                        for v_, dst, dkey in ((0, modS[:, :], "modS"), (1, modP[:, l * 96:(l + 1) * 96], ("modP", l))):
                            def mchunk(n, v_=v_):
                                return mtv[:, n * 16:(n + 1) * 16, v_]

                            def ng(i):
                                return vec[:, l * 64 + i * 16: l * 64 + (i + 1) * 16]
                            steps = [
                                (dst[:, 0:16], mchunk(1), ng(0), True),
                                (dst[:, 16:32], mchunk(0), None, False),
                                (dst[:, 32:48], mchunk(2), ng(1), False),
                                (dst[:, 48:64], mchunk(4), ng(2), True),
                                (dst[:, 64:80], mchunk(3), None, False),
                                (dst[:, 80:96], mchunk(5), ng(3), False),
                            ]
                            for (o_, m_, g_, plus1) in steps:
                                if g_ is None:
                                    B.op("dve", lambda e, o_=o_, m_=m_: e.tensor_copy(out=o_, in_=m_),
                                         r=[("mt", v_)], w=[dkey])
                                else:
                                    B.op("dve", lambda e, o_=o_, m_=m_, g_=g_, plus1=plus1: e.scalar_tensor_tensor(
                                        out=o_, in0=m_, scalar=(1.0 if plus1 else 0.0), in1=g_,
                                        op0=ALU.add, op1=ALU.mult), r=[("mt", v_), "vec"], w=[dkey])
                                    B.op("dve", lambda e, o_=o_: e.tensor_scalar(out=o_, in0=o_, scalar1=SQD,
                                                                                scalar2=None, op0=ALU.mult),
                                         r=[dkey], w=[dkey])
                    B.stage('ada' + ('s' if is_s else 'p'))
                    mod = modS[:, :] if is_s else modP[:, l * 96:(l + 1) * 96]
                    mkey = "modS" if is_s else ("modP", l)
                    gA, shA, gtA = mod[:, 0:16], mod[:, 16:32], mod[:, 32:48]
                    gM, shM, gtM = mod[:, 48:64], mod[:, 64:80], mod[:, 80:96]

                    vb = L_FULL * 64
                    B.op("dve", lambda e, l=l: e.tensor_scalar(out=vec2[:, 0:1], in0=vec[:, vb + l:vb + l + 1],
                                                          scalar1=SQH, scalar2=None, op0=ALU.mult),
                         r=["vec"], w=["vec2"])
                    B.op("dve", lambda e, l=l: e.tensor_scalar(out=vec2[:, 1:2],
                                                          in0=vec[:, vb + L_FULL + l:vb + L_FULL + l + 1],
                                                          scalar1=SQH, scalar2=None, op0=ALU.mult),
                         r=["vec"], w=["vec2"])
                    B.op("dve", lambda e, l=l, lam_init=lam_init: e.tensor_scalar(out=vec2[:, 2:3],
                                                          in0=vec[:, vb + 2 * L_FULL + l:vb + 2 * L_FULL + l + 1],
                                                          scalar1=SQH * (1.0 - lam_init), scalar2=None, op0=ALU.mult),
                         r=["vec"], w=["vec2"])
                    B.dma("sp", lamt[:, :], lam_in[:, l * 256:(l + 1) * 256], "lamt", w=["lamt"])
                    B.op("dve", lambda e: e.tensor_tensor(out=lamw[:, 0:64], in0=lamt[:, 0:64], in1=lamt[:, 64:128],
                                                          op=ALU.mult), r=["lamt"], w=["lamw"])
                    B.op("dve", lambda e: e.tensor_tensor(out=lamw[:, 64:128], in0=lamt[:, 128:192],
                                                          in1=lamt[:, 192:256], op=ALU.mult), r=["lamt"], w=["lamw"])
                    B.op("dve", lambda e: e.reduce_sum(out=lamw[:, 128:130],
                                                       in_=lamw[:, 0:128].rearrange("p (a b) -> p a b", a=2),
                                                       axis=mybir.AxisListType.X), r=["lamw"], w=["lamw2"])
                    B.op("act", lambda e: e.activation(out=lamw[:, 130:132], in_=lamw[:, 128:130], func=AF.Exp),
                         r=["lamw2"], w=["lamw3"])
                    B.op("dve", lambda e: e.tensor_tensor(out=lamw[:, 132:133], in0=lamw[:, 131:132],
                                                          in1=lamw[:, 130:131], op=ALU.subtract),
                         r=["lamw3"], w=["lamw4"])
                    B.op("dve", lambda e, lam_init=lam_init: e.tensor_scalar(out=vec2[:, 3:4], in0=lamw[:, 132:133],
                                                          scalar1=-lam_init, scalar2=None, op0=ALU.add),
                         r=["lamw4"], w=["vec2"])
                    gq1, gk1, dg1, nlam = vec2[:, 0:1], vec2[:, 1:2], vec2[:, 2:3], vec2[:, 3:4]

                    B.slab_wide = (not is_s)
                    B.S.barrier()
                    for blk in range(NB):
                        B.prenorm(xv, blk, hS, "h", gA, shA, mkey)
                    if l == 0:
                        B.tap("h0" + ("s" if is_s else "p"), hS, [("h", c, b_) for c in range(16) for b_ in range(NB)],
                              [128, 16, T])

                    B.stage('prenorm' + ('s' if is_s else 'p'))

                    def h_rhs(k, blk):
                        return hS[:, k, blk * 512:(blk + 1) * 512], [("h", k, blk)]

                    def h_lhs(k, tc):
                        return hS[:, k, tc * 128:(tc + 1) * 128], [("h", k, tc // 4)]

                    def load_ctx(knm, vnm, nh):
                        hw = nh * 128
                        B.dma("sp", kcs[:, :, 0:hw], cache[knm][l].rearrange("(j p) w -> p j w", p=128), "kcs",
                              w=["kcs"])
                        B.dma("pool", vv[:, 8:10, 0:hw], cache[vnm][l].rearrange("(j p) w -> p j w", p=128), "vctx",
                              w=[("v", 8), ("v", 9)], post_barrier=True)
                        for h0 in range(0, nh, 2):
                            mb = B.misc_bank()
                            outb = B.ps[mb]

                            def fn(e, h0=h0, outb=outb):
                                ins = None
                                for hh in range(2):
                                    for j in range(2):
                                        o = outb[:, (hh * 2 + j) * 128:(hh * 2 + j + 1) * 128]
                                        ins = e.transpose(o, kcs[:, j, (h0 + hh) * 128:(h0 + hh + 1) * 128],
                                                          ident[:, :])
                                return ins
                            B.op("pe", fn, r=["kcs", "ident"], w=[("ps", mb)])
                            dst = kT[:, h0:h0 + 2, T:T + 256]
                            src = outb[:, :].rearrange("p (h t) -> p h t", h=2)
                            B.copy_op(B.evac_eng(), dst, src, r=[("ps", mb)], w=[("kT", h0, 9), ("kT", h0 + 1, 9)])

                    def evac_plain(dst_fn, wkey_fn, scale=None, f32_fn=None):
                        def ev(mc, blk, bank_ap, bkey):
                            dst = dst_fn(mc, blk)
                            if f32_fn is not None:
                                d2, k2 = f32_fn(mc, blk)
                                B.op("dve", lambda e: e.tensor_copy(out=d2, in_=bank_ap), r=[bkey], w=k2)
                                B.op("act", lambda e: e.activation(out=dst, in_=d2, func=AF.Copy), r=k2,
                                     w=wkey_fn(mc, blk))
                                return
                            if scale is None:
                                B.copy_op(B.evac_eng(), dst, bank_ap, r=[bkey], w=wkey_fn(mc, blk))
                            else:
                                B.op("act", lambda e: e.activation(out=dst, in_=bank_ap, func=AF.Identity,
                                                                   scale=float(scale)), r=[bkey], w=wkey_fn(mc, blk))
                        return ev

                    def bs(blk):
                        return slice(blk * 512, (blk + 1) * 512)

                    def v_evac(nh, out_nm):
                        def ev(tc, coff, gw, bank_ap, bkey):
                            dst = vv[:, tc, coff:coff + gw]
                            if is_s:
                                B.copy_op(B.evac_eng(), dst, bank_ap, r=[bkey], w=[("v", tc)])
                            else:
                                i = B.ostg_rr % 4
                                B.ostg_rr += 1
                                st = ostg[i]
                                B.op("dve", lambda e: e.tensor_copy(out=st[:, 0:gw], in_=bank_ap),
                                     r=[bkey], w=[("ostg", i)])
                                B.op("act", lambda e: e.activation(out=dst, in_=st[:, 0:gw], func=AF.Copy),
                                     r=[("ostg", i)], w=[("v", tc)])
                                seq, tr = tc // 2, (tc % 2) * 128
                                B.dma("sp", okv[out_nm][seq, l, tr:tr + 128, coff:coff + gw], st[:, 0:gw],
                                      "ostg%d" % i, r=[("ostg", i)])
                        return ev

                    def k_out(nh, out_nm, src_f32, skey_fn):
                        for tc in range(4):
                            mb = B.misc_bank()
                            outb = B.ps[mb]

                            def fn(e, tc=tc, outb=outb):
                                ins = None
                                for h in range(nh):
                                    ins = e.transpose(outb[:, h * 128:(h + 1) * 128],
                                                      src_f32[:, h, tc * 128:(tc + 1) * 128], ident[:, :])
                                return ins
                            B.op("pe", fn, r=[skey_fn(h) for h in range(nh)] + ["ident"], w=[("ps", mb)])
                            i = B.ostg_rr % 4
                            B.ostg_rr += 1
                            st = ostg[i]
                            hw = nh * 128
                            B.copy_op(B.evac_eng(), st[:, 0:hw], outb[:, 0:hw], r=[("ps", mb)], w=[("ostg", i)])
                            seq, tr = tc // 2, (tc % 2) * 128
                            B.dma("sp", okv[out_nm][seq, l, tr:tr + 128, :], st[:, 0:hw], "ostg%d" % i,
                                  r=[("ostg", i)])

                    def mul_segs(dst, segs, R, rkey, wkeys):
                        for (bk, lo, hi) in segs:
                            B.op("dve", lambda e, bk=bk, lo=lo, hi=hi: e.tensor_tensor(
                                out=dst[:, lo:hi], in0=B.ps[bk][:, lo:hi], in1=R[:, lo:hi], op=ALU.mult),
                                r=[("ps", bk), rkey], w=wkeys)

                    def finish_head(sum_b, o_bs, slot, blk, bw=512):
                        R, rkey = B.recip(sum_b)
                        dst = QO[:, slot, bs(blk)]
                        segs = [(o_bs[0], 0, 512)] if len(o_bs) == 1 else [(o_bs[0], 0, 256), (o_bs[1], 256, 512)]
                        mul_segs(dst, segs, R, rkey, [("qo", slot, blk)])

                    acc_rr = [0]

                    def acc_pair():
                        a = acc_rr[0] % 2
                        acc_rr[0] += 1
                        return (4, 5) if a == 0 else (6, 7)

                    if is_s:
                        B.dma("pool", tab[:, :], tbin[l], "tab", w=["tab"], post_barrier=True)
                        load_ctx("nak", "nav", 4)
                    B.linear(w_in[l], D, C_NAQ, 512, h_rhs, NB,
                             evac_plain(lambda mc, blk: QO[:, mc, bs(blk)], lambda mc, blk: [("qo", mc, blk)],
                                        scale=SCALE_H))
                    B.linear(w_in[l], D, C_NAK, 512, h_rhs, NB,
                             evac_plain(lambda mc, blk: kT[:, mc, bs(blk)], lambda mc, blk: [("kT", mc, blk)],
                                        f32_fn=None if is_s else (lambda mc, blk: (kf32[:, mc, :], [("kf", mc)]))))
                    B.linear_swapped(w_in[l], C_NAV, 512, h_lhs, T // 128, v_evac(4, "nav"))
                    if not is_s:
                        k_out(4, "nak", kf32, lambda h: ("kf", h))
                    B.stage('na_proj' + ('s' if is_s else 'p'))
                    if is_s:
                        for h in range(4):
                            for blk in range(2):
                                chunks = list(range(0, 6)) if blk == 0 else list(range(2, 8))
                                chunks = chunks + [8, 9]
                                sum_b, o_b = acc_pair()

                                def st_fn(j, h=h, blk=blk, chunks=chunks):
                                    i = chunks[j]
                                    mm = [(0, 512, kT[:, h, i * 128:(i + 1) * 128], QO[:, h, bs(blk)],
                                           [("kT", h, i // 4 if i < 8 else 9), ("qo", h, blk)])]
                                    if i < 8:
                                        e0 = 10 - 2 * i + 8 * blk
                                        mm.append((0, 512, vlr[0:96, i * 128:(i + 1) * 128],
                                                   vlr[0:96, 1024 + blk * 512:1024 + (blk + 1) * 512], ["vlr"]))
                                        mm.append((0, 512, ident_bf[:, :],
                                                   tab[:, h * TB_E * 64 + e0 * 64:h * TB_E * 64 + (e0 + 8) * 64],
                                                   ["tab", "identbf"]))
                                    return mm

                                def pv_fn(j, h=h, chunks=chunks, o_b=o_b):
                                    i = chunks[j]
                                    return [(0, 0, 512, vv[:, i, h * 128:(h + 1) * 128], [("v", i)],
                                             j == 0, j == len(chunks) - 1)]
                                B.attn_accumulate(len(chunks), st_fn, pv_fn, 1.0, sum_b, [o_b])
                                finish_head(sum_b, [o_b], h, blk)
                    else:
                        for h in range(4):
                            sum_b, o_b, o_b2 = 4, 5, 6

                            def st_fn(j, h=h):
                                return [(0, 256, kT[:, h, j * 128:(j + 1) * 128], QO[:, h, 0:256],
                                         [("kT", h, 0), ("qo", h, 0)]),
                                        (256, 512, kT[:, h, 256 + j * 128:256 + (j + 1) * 128], QO[:, h, 256:512],
                                         [("kT", h, 0), ("qo", h, 0)])]

                            def pv_fn(j, h=h):
                                return [(0, 0, 256, vv[:, j, h * 128:(h + 1) * 128], [("v", j)], j == 0, j == 1),
                                        (1, 256, 512, vv[:, 2 + j, h * 128:(h + 1) * 128], [("v", 2 + j)],
                                         j == 0, j == 1)]
                            B.attn_accumulate(2, st_fn, pv_fn, 1.0, sum_b, [o_b, o_b2], st_pool=(0, 1, 2, 3))
                            finish_head(sum_b, [o_b, o_b2], h, 0)
                    if l == 0:
                        B.tap("ona" + ("s" if is_s else "p"), QO[:, 0:4, :],
                              [("qo", h, b_) for h in range(4) for b_ in range(NB)], [128, 4, T])

                    B.stage('na_attn' + ('s' if is_s else 'p'))
                    def normrope_evac(dst_fn, wkey_fn, gain, do_norm, rope_idx, f32_fn=None):
                        def ev(mc, blk, bank_ap, bkey):
                            dst = dst_fn(mc, blk)
                            wk_ = wkey_fn(mc, blk)
                            q32, qkey = B.work()
                            if do_norm:
                                et, ekey = B.etile()
                                B.op("dve", lambda e: e.tensor_copy(out=q32[:, :], in_=bank_ap), r=[bkey], w=[qkey])
                                B.op("act", lambda e: e.activation(out=et[:, :], in_=q32[:, :], func=AF.Square),
                                     r=[qkey], w=[ekey])
                                rs, rkey = B.stats_rstd(lambda c: (et[:, :], [ekey]), 1, HD * EPS)
                                need32 = (rope_idx is not None) or (f32_fn is not None)
                                tgt = q32[:, :] if need32 else dst
                                B.op("dve", lambda e: e.scalar_tensor_tensor(out=tgt, in0=q32[:, :], scalar=gain,
                                                                             in1=rs, op0=ALU.mult, op1=ALU.mult),
                                     r=[qkey, rkey, "vec2"], w=[qkey] if need32 else wk_)
                                if not need32:
                                    return
                            else:
                                B.copy_op("act", q32[:, :], bank_ap, r=[bkey], w=[qkey])
                            if f32_fn is not None:
                                d2, k2 = f32_fn(mc, blk)
                                B.op("act", lambda e: e.activation(out=d2, in_=q32[:, :], func=AF.Copy),
                                     r=[qkey], w=k2)
                            if rope_idx is None:
                                B.copy_op("dve", dst, q32[:, :], r=[qkey], w=wk_)
                                return
                            mb = B.misc_bank()
                            rb = B.ps[mb][:, 0:512]
                            pm = perm[:, rope_idx * 128:(rope_idx + 1) * 128]
                            B.op("pe", lambda e: e.matmul(rb, lhsT=pm, rhs=q32[:, :], start=True, stop=True),
                                 r=[qkey, "perm"], w=[("ps", mb)])
                            cosv = tabf[:, blk * 512:(blk + 1) * 512]
                            sinv = tabf[:, 1024 + blk * 512:1024 + (blk + 1) * 512]
                            t1, t1k = B.work()
                            t2, t2k = B.work()
                            B.op("dve", lambda e: e.tensor_tensor(out=t1[:, :], in0=q32[:, :], in1=cosv, op=ALU.mult),
                                 r=[qkey, "tab"], w=[t1k])
                            B.op("dve", lambda e: e.tensor_tensor(out=t2[:, :], in0=rb, in1=sinv, op=ALU.mult),
                                 r=[("ps", mb), "tab"], w=[t2k])
                            B.op("dve", lambda e: e.tensor_tensor(out=dst, in0=t1[:, :], in1=t2[:, :], op=ALU.add),
                                 r=[t1k, t2k], w=wk_)
                        return ev

                    if is_s:
                        B.dma("sp", tabf[:, 0:2048], rope_in[0], "tab", w=["tab"], r=[])
                        load_ctx("gk", "gv", 2)
                    B.linear(w_in[l], D, C_GQ, 1024, h_rhs, NB,
                             normrope_evac(lambda mc, blk: QO[:, 4 + mc, bs(blk)],
                                           lambda mc, blk: [("qo", 4 + mc, blk)], gq1, True, 0 if is_s else None))
                    B.linear(w_in[l], D, C_GK, 256, h_rhs, NB,
                             normrope_evac(lambda mc, blk: kT[:, mc, bs(blk)], lambda mc, blk: [("kT", mc, blk)],
                                           gk1, True, 0 if is_s else None,
                                           f32_fn=None if is_s else (lambda mc, blk: (kf32[:, mc, :], [("kf", mc)]))))
                    B.linear_swapped(w_in[l], C_GV, 256, h_lhs, T // 128, v_evac(2, "gv"))
                    if not is_s:
                        k_out(2, "gk", kf32, lambda h: ("kf", h))
                    B.stage('gqa_proj' + ('s' if is_s else 'p'))
                    for qh in range(8):
                        kvh = qh // 4
                        if is_s:
                            for blk in range(2):
                                sum_b, o_b = acc_pair()

                                def st_fn(j, qh=qh, kvh=kvh, blk=blk):
                                    return [(0, 512, kT[:, kvh, j * 128:(j + 1) * 128], QO[:, 4 + qh, bs(blk)],
                                             [("kT", kvh, j // 4 if j < 8 else 9), ("qo", 4 + qh, blk)])]

                                def pv_fn(j, kvh=kvh):
                                    return [(0, 0, 512, vv[:, j, kvh * 128:(kvh + 1) * 128], [("v", j)],
                                             j == 0, j == 9)]
                                B.attn_accumulate(10, st_fn, pv_fn, SCALE_H, sum_b, [o_b])
                                finish_head(sum_b, [o_b], 4 + qh, blk)
                        else:
                            sum_b, o_b, o_b2 = 4, 5, 6

                            def st_fn(j, qh=qh, kvh=kvh):
                                return [(0, 256, kT[:, kvh, j * 128:(j + 1) * 128], QO[:, 4 + qh, 0:256],
                                         [("kT", kvh, 0), ("qo", 4 + qh, 0)]),
                                        (256, 512, kT[:, kvh, 256 + j * 128:256 + (j + 1) * 128],
                                         QO[:, 4 + qh, 256:512], [("kT", kvh, 0), ("qo", 4 + qh, 0)])]

                            def pv_fn(j, kvh=kvh):
                                return [(0, 0, 256, vv[:, j, kvh * 128:(kvh + 1) * 128], [("v", j)], j == 0, j == 1),
                                        (1, 256, 512, vv[:, 2 + j, kvh * 128:(kvh + 1) * 128], [("v", 2 + j)],
                                         j == 0, j == 1)]
                            B.attn_accumulate(2, st_fn, pv_fn, SCALE_H, sum_b, [o_b, o_b2])
                            finish_head(sum_b, [o_b, o_b2], 4 + qh, 0)
                    if l == 0:
                        B.tap("ogq" + ("s" if is_s else "p"), QO[:, 4:12, :],
                              [("qo", h, b_) for h in range(4, 12) for b_ in range(NB)], [128, 8, T])

                    B.stage('gqa_attn' + ('s' if is_s else 'p'))
                    if is_s:
                        B.dma("sp", tabf[:, 0:2048], rope_in[1], "tab", w=["tab"])
                        load_ctx("dk", "dv", 4)
                    B.linear(w_in[l], D, C_DQ, 512, h_rhs, NB,
                             normrope_evac(lambda mc, blk: QO[:, 12 + mc, bs(blk)],
                                           lambda mc, blk: [("qo", 12 + mc, blk)], None, False, 1 if is_s else None))
                    B.linear(w_in[l], D, C_DK, 512, h_rhs, NB,
                             normrope_evac(lambda mc, blk: kT[:, mc, bs(blk)], lambda mc, blk: [("kT", mc, blk)],
                                           None, False, 1 if is_s else None,
                                           f32_fn=None if is_s else (lambda mc, blk: (kf32[:, mc, :], [("kf", mc)]))))
                    B.linear_swapped(w_in[l], C_DV, 512, h_lhs, T // 128, v_evac(4, "dv"))
                    if not is_s:
                        k_out(4, "dk", kf32, lambda h: ("kf", h))
                    B.stage('diff_proj' + ('s' if is_s else 'p'))
                    for h in range(4):
                        for blk in range(NB):
                            nchk = 10 if is_s else 2
                            for mp in range(2):
                                ps_ = slice(mp * 64, (mp + 1) * 64)
                                if is_s:
                                    sum_b, o_bs = ((4, [5]) if mp == 0 else (6, [7]))
                                else:
                                    sum_b, o_bs = ((2, [3, 4]) if mp == 0 else (5, [6, 7]))
                                if is_s:
                                    def st_fn(j, h=h, blk=blk, ps_=ps_):
                                        return [(0, 512, kT[ps_, h, j * 128:(j + 1) * 128], QO[ps_, 12 + h, bs(blk)],
                                                 [("kT", h, j // 4 if j < 8 else 9), ("qo", 12 + h, blk)])]

                                    def pv_fn(j, h=h):
                                        return [(0, 0, 512, vv[:, j, h * 128:(h + 1) * 128], [("v", j)],
                                                 j == 0, j == 9)]
                                else:
                                    def st_fn(j, h=h, ps_=ps_):
                                        return [(0, 256, kT[ps_, h, j * 128:(j + 1) * 128], QO[ps_, 12 + h, 0:256],
                                                 [("kT", h, 0), ("qo", 12 + h, 0)]),
                                                (256, 512, kT[ps_, h, 256 + j * 128:256 + (j + 1) * 128],
                                                 QO[ps_, 12 + h, 256:512], [("kT", h, 0), ("qo", 12 + h, 0)])]

                                    def pv_fn(j, h=h):
                                        return [(0, 0, 256, vv[:, j, h * 128:(h + 1) * 128], [("v", j)],
                                                 j == 0, j == 1),
                                                (1, 256, 512, vv[:, 2 + j, h * 128:(h + 1) * 128], [("v", 2 + j)],
                                                 j == 0, j == 1)]
                                B.attn_accumulate(nchk, st_fn, pv_fn, SCALE_DQ, sum_b, o_bs,
                                                  st_pool=(0, 1, 2, 3) if is_s else (0, 1))
                            R0, r0k = B.recip(4 if is_s else 2)
                            R1, r1k = B.recip(6 if is_s else 5)
                            t0, t0k = B.work()
                            t1, t1k = B.work()
                            if is_s:
                                segs0, segs1 = [(5, 0, 512)], [(7, 0, 512)]
                            else:
                                segs0, segs1 = [(3, 0, 256), (4, 256, 512)], [(6, 0, 256), (7, 256, 512)]
                            mul_segs(t0, segs0, R0, r0k, [t0k])
                            mul_segs(t1, segs1, R1, r1k, [t1k])
                            B.op("dve", lambda e, t0=t0, t1=t1: e.scalar_tensor_tensor(
                                out=t0[:, :], in0=t1[:, :], scalar=nlam, in1=t0[:, :], op0=ALU.mult, op1=ALU.add),
                                r=[t0k, t1k, "vec2"], w=[t0k])
                            et, ekey = B.etile()
                            B.op("act", lambda e, et=et, t0=t0: e.activation(out=et[:, :], in_=t0[:, :],
                                                                              func=AF.Square), r=[t0k], w=[ekey])
                            rs, rkey = B.stats_rstd(lambda c, et=et, ekey=ekey: (et[:, :], [ekey]), 1, HD * EPS)
                            dst = QO[:, 12 + h, bs(blk)]
                            B.op("dve", lambda e, t0=t0, rs=rs, dst=dst: e.scalar_tensor_tensor(
                                out=dst, in0=t0[:, :], scalar=dg1, in1=rs, op0=ALU.mult, op1=ALU.mult),
                                r=[t0k, rkey, "vec2"], w=[("qo", 12 + h, blk)])
                    if l == 0:
                        B.tap("odf" + ("s" if is_s else "p"), QO[:, 12:16, :],
                              [("qo", h, b_) for h in range(12, 16) for b_ in range(NB)], [128, 4, T])

                    B.stage('diff_attn' + ('s' if is_s else 'p'))
                    B.S.barrier()
                    for blk in range(NB):
                        def o_rhs(k, blk_, blk=blk):
                            return QO[:, k, bs(blk)], [("qo", k, blk)]

                        def y_evac(mc, blk_, bank_ap, bkey):
                            B.copy_op(B.evac_eng(), yblk[:, mc, :], bank_ap, r=[bkey], w=[("y", mc)])
                        B.linear(w_out[l], D, 0, D, o_rhs, 1, y_evac)
                        B.postnorm_residual(xv, blk, yblk, "y", sqA, "sqA", gtA, mkey)
                    if l == 0:
                        B.tap("x1" + ("s" if is_s else "p"), xv, [("x", c, b_) for c in range(16) for b_ in range(NB)],
                              [128, 16, T])

                    B.stage('wout' + ('s' if is_s else 'p'))
                    B.S.barrier()
                    B.slab_wide = True
                    for blk in range(NB):
                        B.prenorm(xv, blk, hblk, "hb", gM, shM, mkey)

                        def hb_rhs(k, blk_):
                            return hblk[:, k, :], [("hb", k, 0)]
                        for half in range(2):
                            def u_evac(mc, blk_, bank_ap, bkey):
                                dst = u2[:, mc, :]
                                tr_, trk_ = B.work()
                                B.op("dve", lambda e: e.tensor_scalar(
                                    out=tr_[:, :], in0=bank_ap, scalar1=0.0, scalar2=None, op0=ALU.max),
                                    r=[bkey], w=[trk_])
                                B.op("act", lambda e: e.activation(out=dst, in_=tr_[:, :], func=AF.Square),
                                     r=[trk_], w=[("u2", mc)])
                            B.linear(w_up[l], D, half * 4096, 4096, hb_rhs, 1, u_evac)

                            def u_rhs(k, blk_):
                                return u2[:, k, :], [("u2", k)]

                            def yd_evac(mc, blk_, bank_ap, bkey, half=half):
                                if half == 0:
                                    B.copy_op("act", yblk[:, mc, :], bank_ap, r=[bkey], w=[("y", mc)])
                                else:
                                    yo = yblk[:, mc, :]
                                    B.op("dve", lambda e: e.tensor_tensor(out=yo, in0=bank_ap, in1=yo, op=ALU.add),
                                         r=[bkey, ("y", mc)], w=[("y", mc)])
                            B.linear(w_down[l, half * 4096:(half + 1) * 4096, :], 4096, 0, D, u_rhs, 1, yd_evac)
                        B.postnorm_residual(xv, blk, yblk, "y", hblk, "hb", gtM, mkey)

                for c4 in range(4):
                    B.dma("sp", xout[512 * c4:512 * (c4 + 1), :].rearrange("(c p) t -> p c t", p=128),
                          xv[:, 4 * c4:4 * c4 + 4, :], "xout",
                          r=[("x", c, blk) for c in range(4 * c4, 4 * c4 + 4) for blk in range(NB)])

            try:
                phase(True)
                phase(False)
            except _Stop:
                pass

            with ExitStack() as es2:
                eng_sems = {k: es2.enter_context(nc.semaphore("sem_" + k)) for k in Sched.STREAMS}
                dma_sems = {k: es2.enter_context(nc.semaphore("dsem_" + str(i)))
                            for i, k in enumerate(sorted(self.dma_keys))}
                block = es2.enter_context(nc.Block())
                B.S.emit(nc, {"pe": block.tensor, "act": block.scalar, "dve": block.vector,
                              "pool": block.gpsimd, "sp": block.sync}, eng_sems, dma_sems)
        return nc


def _consts():
    ident = np.eye(128, dtype=np.float32)
    perm = np.zeros((128, 256), dtype=np.float32)
    for m in range(128):
        pg = m + 32 if (m % 64) < 32 else m - 32
        perm[pg, m] = 1.0
        pd = m + 16 if (m % 32) < 16 else m - 16
        perm[pd, 128 + m] = 1.0
    t = np.arange(1024)
    row = (t // GRID_W).astype(np.float32)
    col = (t % GRID_W).astype(np.float32)
    rope = np.zeros((2, 128, 2048), dtype=np.float32)
    for p in range(128):
        i = p % 32
        inv = np.float32(10000.0) ** (-np.float32(i) / np.float32(32))
        pos = row if p < 64 else col
        ang = (pos * inv).astype(np.float32)
        sgn = -1.0 if (p % 64) < 32 else 1.0
        rope[0, p, 0:1024] = np.cos(ang)
        rope[0, p, 1024:2048] = sgn * np.sin(ang)
        i = p % 16
        inv = np.float32(10000.0) ** (-np.float32(i) / np.float32(16))
        pos = row if (p % 64) < 32 else col
        ang = (pos * inv).astype(np.float32)
        sgn = -1.0 if (p % 32) < 16 else 1.0
        rope[1, p, 0:1024] = np.cos(ang)
        rope[1, p, 1024:2048] = sgn * np.sin(ang)
    vl = np.zeros((128, 1024), dtype=np.float32)
    vr = np.zeros((128, 1024), dtype=np.float32)
    s = np.arange(1024)
    rs, cs = s // 64, s % 64
    for k in range(16):
        vl[k, :] = (rs == k)
    for k in range(64):
        vl[16 + k, :] = (cs == k)
    rq, cq = rs, cs
    r0 = np.clip(rq - 4, 0, 8)
    ws = np.clip(cq - 8, 0, 48)
    for k in range(16):
        vr[k, :] = np.where((k >= r0) & (k < r0 + 8), 0.0, NEG)
    for k in range(64):
        vr[16 + k, :] = np.where((k >= ws) & (k < ws + 16), 0.0, NEG)
    vlr = np.concatenate([vl, vr], axis=1)
    return ident, perm, rope, vlr


def _tb_index():
    a = (np.arange(128) // 64)[:, None, None]
    cs = (np.arange(128) % 64)[:, None, None]
    e = np.arange(TB_E)[None, :, None]
    cq = np.arange(64)[None, None, :]
    d = a + 17 - e + 0 * cq
    dc = cs - cq + 15 + 0 * e
    ok = (d >= 0) & (d <= 14) & (dc >= 0) & (dc <= 30)
    return np.clip(d, 0, 14), np.clip(dc, 0, 30), ok


_CACHE = {}


def _get_program(depth, taps=(), stop=None):
    key = (depth, tuple(taps), stop)
    if key not in _CACHE:
        b = Builder(depth, taps, stop)
        nc = b.build()
        _CACHE[key] = (nc, b)
    return _CACHE[key]


def kernel(x_prompt, x_sample, c, cache_na_k, cache_na_v, cache_gqa_k, cache_gqa_v, cache_diff_k, cache_diff_v,
           c_ctx, w_ada, b_ada, norm_g, w_in, w_out, na_rpb, gqa_q_g, gqa_k_g, diff_lam, diff_g, w_up, w_down,
           _depth=L_FULL, _taps=(), _stop=None):
    f = lambda a: np.ascontiguousarray(np.asarray(a, dtype=np.float32))
    x_prompt, x_sample, c = f(x_prompt), f(x_sample), f(c)
    c_ctx = f(c_ctx)
    w_ada, w_in, w_out, w_up, w_down = f(w_ada), f(w_in), f(w_out), f(w_up), f(w_down)
    b_ada, norm_g, na_rpb = f(b_ada), f(norm_g), f(na_rpb)
    gqa_q_g, gqa_k_g, diff_lam, diff_g = f(gqa_q_g), f(gqa_k_g), f(diff_lam), f(diff_g)
    caches = {"nak": f(cache_na_k), "nav": f(cache_na_v), "gk": f(cache_gqa_k), "gv": f(cache_gqa_v),
              "dk": f(cache_diff_k), "dv": f(cache_diff_v)}
    nc, bld = _get_program(_depth, _taps, _stop)
    ident, perm, rope, vlr = _consts()
    didx, cidx, ok = _tb_index()
    L = L_FULL
    tb = np.zeros((L, 128, 4, TB_E, 64), dtype=np.float32)
    for h in range(4):
        gth = na_rpb[:, h][:, didx, cidx]
        tb[:, :, h] = np.where(ok[None], gth, np.float32(0.0))
    tb = np.ascontiguousarray(tb.reshape(L, 128, 4 * TB_E * 64))
    badaT = np.ascontiguousarray(b_ada.reshape(L, 96, 128).transpose(0, 2, 1))
    vecs = np.zeros((128, L * 64 + 3 * L), dtype=np.float32)
    vecs[:, 0:L * 64] = norm_g.reshape(L, 4, 16, 128).transpose(3, 0, 1, 2).reshape(128, L * 64)
    vecs[:, L * 64:L * 64 + L] = gqa_q_g.T
    vecs[:, L * 64 + L:L * 64 + 2 * L] = gqa_k_g.T
    vecs[:, L * 64 + 2 * L:L * 64 + 3 * L] = diff_g.T
    lam_in = np.ascontiguousarray(np.broadcast_to(diff_lam.reshape(1, L * 256), (128, L * 256)))
    in_maps = []
    for i in range(8):
        cvec = np.zeros((128, 16, 2), dtype=np.float32)
        cvec[:, :, 0] = c[i].reshape(16, 128).T
        cvec[:, :, 1] = c_ctx.reshape(16, 128).T
        xp = np.concatenate([x_prompt[2 * i], x_prompt[2 * i + 1]], axis=0)
        m = {
            "xsT": np.ascontiguousarray(x_sample[i].T), "xpT": np.ascontiguousarray(xp.T),
            "cvec": np.ascontiguousarray(cvec.reshape(128, 32)),
            "w_ada": w_ada, "w_in": w_in, "w_out": w_out, "w_up": w_up, "w_down": w_down,
            "badaT": badaT, "vecs": vecs, "lam_in": lam_in, "tbin": tb,
            "ident_in": ident, "perm_in": perm, "vlr_in": vlr, "rope_in": rope,
        }
        for nm, arr in caches.items():
            m["c_" + nm] = np.ascontiguousarray(arr[i].reshape(L, 256, -1))
        in_maps.append(m)
    res = run_bass_kernel_spmd(nc, in_maps, core_ids=list(range(8)))
    R = res.results
    y_prompt = np.zeros((16, 256, D), dtype=np.float32)
    y_sample = np.zeros((8, 1024, D), dtype=np.float32)
    outs = {nm: np.zeros((16, L, 256, hw), dtype=np.float32)
            for nm, hw in (("nak", 512), ("nav", 512), ("gk", 256), ("gv", 256), ("dk", 512), ("dv", 512))}
    for i in range(8):
        y_sample[i] = R[i]["ysT"].T
        yp = R[i]["ypT"].T
        y_prompt[2 * i] = yp[0:256]
        y_prompt[2 * i + 1] = yp[256:512]
        for nm in outs:
            outs[nm][2 * i:2 * i + 2] = R[i]["o_" + nm]
    if _taps:
        kernel.last_taps = [{k: R[i]["tap_" + k] for k in bld.tap_outs} for i in range(8)]
    return (y_prompt, y_sample,
            outs["nak"].reshape(16, L, 256, 4, 128), outs["nav"].reshape(16, L, 256, 4, 128),
            outs["gk"].reshape(16, L, 256, 2, 128), outs["gv"].reshape(16, L, 256, 2, 128),
            outs["dk"].reshape(16, L, 256, 4, 128), outs["dv"].reshape(16, L, 256, 4, 128))
```

```python
import math
from contextlib import ExitStack

import numpy as np
import concourse.bass as bass
import concourse.mybir as mybir
from concourse.bass_utils import run_bass_kernel_spmd

F32 = mybir.dt.float32
BF16 = mybir.dt.bfloat16
AF = mybir.ActivationFunctionType
ALU = mybir.AluOpType

D = 2048
DC = 16
L_FULL = 4
HD = 128
EPS = 1e-6
D_IN = 4608
D_FF = 8192
GRID_W = 64
NEG = -30000.0
SQD = math.sqrt(2048.0)
SQH = math.sqrt(128.0)
SCALE_H = 128.0 ** -0.5
SCALE_DQ = 64.0 ** -0.5
TB_E = 22

C_NAQ, C_NAK, C_NAV = 0, 512, 1024
C_GQ, C_GK, C_GV = 1536, 2560, 2816
C_DQ, C_DK, C_DV = 3072, 3584, 4096


class Op:
    __slots__ = ("eng", "fn", "dma", "dma_val", "signal", "deps", "cnt")


class Sched:
    STREAMS = ("pe", "act", "dve", "pool", "sp")

    def __init__(self):
        self.streams = {k: [] for k in self.STREAMS}
        self.res = {}
        self.dma_cum = {}
        self.bar_deps = {k: [] for k in self.STREAMS}
        self.last_barrier = []
        self.sp_latest = {}
        self.nops = 0

    def add(self, eng, fn, r=(), w=(), dma=None, post_barrier=False):
        op = Op()
        op.eng, op.fn, op.dma, op.signal, op.deps, op.cnt = eng, fn, dma, False, [], 0
        op.dma_val = 0
        cand = []
        for key in r:
            st = self.res.get(key)
            if st is not None and st[0] is not None:
                cand.append((st[0], "raw"))
        for key in w:
            st = self.res.get(key)
            if st is not None:
                if st[0] is not None:
                    cand.append((st[0], "waw"))
                for rd in st[1]:
                    cand.append((rd, "war"))
        for p in self.bar_deps[eng]:
            cand.append((p, "bar"))
        self.bar_deps[eng] = []
        if post_barrier:
            for p in self.last_barrier:
                cand.append((p, "bar"))
        seen = set()
        for p, kind in cand:
            if id(p) in seen:
                continue
            if p.dma is None and p.eng == eng:
                if eng == "pe" and dma is None:
                    continue
            seen.add(id(p))
            if p.dma is not None:
                op.deps.append((p, self.dma_cum[p.dma]))
            else:
                p.signal = True
                op.deps.append((p, None))
        for key in r:
            st = self.res.setdefault(key, [None, []])
            st[1].append(op)
        for key in w:
            self.res[key] = [op, []]
        if dma is not None:
            self.dma_cum[dma] = self.dma_cum.get(dma, 0) + 16
            op.dma_val = self.dma_cum[dma]
            if eng == "sp":
                self.sp_latest[dma] = op
        self.streams[eng].append(op)
        self.nops += 1
        return op

    def barrier(self, engs=("pe", "act", "dve", "sp")):
        lasts = []
        for e in engs:
            if self.streams[e]:
                lasts.append(self.streams[e][-1])
        for k_, p_ in self.sp_latest.items():
            if p_ not in lasts:
                lasts.append(p_)
        self.sp_latest = {}
        for e in engs:
            self.bar_deps[e] = [p for p in lasts if p.eng != e or p.dma is not None]
        self.last_barrier = lasts

    def emit(self, nc, block_engines, eng_sems, dma_sems):
        for name in self.STREAMS:
            cnt = 0
            for op in self.streams[name]:
                if op.dma is None and op.signal:
                    cnt += 1
                op.cnt = cnt

        def run_stream(name):
            def body(e):
                waited = {}
                for op in self.streams[name]:
                    for p, v in op.deps:
                        if p.dma is not None:
                            sem, val, sid = dma_sems[p.dma], v, ("d", p.dma)
                        else:
                            sem, val, sid = eng_sems[p.eng], p.cnt, ("e", p.eng)
                        if waited.get(sid, 0) >= val:
                            continue
                        e.wait_ge(sem, val)
                        waited[sid] = val
                    ins = op.fn(e)
                    if op.dma is not None:
                        ins.then_inc(dma_sems[op.dma], 16)
                    elif op.signal:
                        ins.then_inc(eng_sems[name], 1)
                if name == "sp":
                    for key, val in self.dma_cum.items():
                        if waited.get(("d", key), 0) < val:
                            e.wait_ge(dma_sems[key], val)
            return body

        block_engines["pe"](run_stream("pe"))
        block_engines["act"](run_stream("act"))
        block_engines["dve"](run_stream("dve"))
        block_engines["pool"](run_stream("pool"))
        block_engines["sp"](run_stream("sp"))


class _Stop(Exception):
    pass


class Builder:
    def __init__(self, depth=L_FULL, taps=(), stop=None):
        self.stop = stop
        self.depth = depth
        self.taps = set(taps)
        self.tap_outs = {}
        self.nc = bass.Bass("TRN2", target_bir_lowering=False)
        self.S = Sched()
        self.dma_keys = set()
        self.bank_rr = 0
        self.misc_rr = 0
        self.slab_rr = 0
        self.evac_rr = 0
        self.work_rr = 0
        self.e_rr = 0
        self.rst_rr = 0
        self.slab_wide = False
        self.ostg_rr = 0

    def din(self, name, shape):
        return self.nc.dram_tensor(name, list(shape), F32, kind="ExternalInput").ap()

    def dout(self, name, shape):
        return self.nc.dram_tensor(name, list(shape), F32, kind="ExternalOutput").ap()

    def op(self, eng, fn, r=(), w=(), **kw):
        return self.S.add(eng, fn, r=r, w=w, **kw)

    def dma(self, eng, out, in_, key, r=(), w=(), post_barrier=False):
        key = eng + ":" + key
        self.dma_keys.add(key)
        return self.S.add(eng, lambda e, out=out, in_=in_: e.dma_start(out=out, in_=in_), r=r, w=w, dma=key,
                          post_barrier=post_barrier)

    def next_bank(self, pool=(0, 1, 2, 3, 4, 5)):
        b = pool[self.bank_rr % len(pool)]
        self.bank_rr += 1
        return b

    def misc_bank(self):
        b = 6 + (self.misc_rr % 2)
        self.misc_rr += 1
        return b

    def work(self):
        i = self.work_rr % len(self.wk)
        self.work_rr += 1
        return self.wk[i], ("wk", i)

    def etile(self):
        i = self.e_rr % len(self.et)
        self.e_rr += 1
        return self.et[i], ("et", i)

    def evac_eng(self):
        self.evac_rr += 1
        return "act" if self.evac_rr % 2 else "dve"

    def copy_op(self, eng, out, in_, r, w):
        if eng == "act":
            return self.op("act", lambda e: e.activation(out=out, in_=in_, func=AF.Copy), r=r, w=w)
        return self.op("dve", lambda e: e.tensor_copy(out=out, in_=in_), r=r, w=w)

    def stage(self, name):
        if self.stop == name:
            raise _Stop()

    def tap(self, name, ap, rkeys, shape):
        if name not in self.taps:
            return
        o = self.nc.dram_tensor("tap_" + name, list(shape), ap.dtype, kind="ExternalOutput").ap()
        self.tap_outs[name] = (shape, ap.dtype)
        self.dma("sp", o, ap, "tap_" + name, r=rkeys)

    def load_slab(self, wl, k0, kc, c0, gw):
        npool = 4 if self.slab_wide else 2
        j = self.slab_rr % npool
        self.slab_rr += 1
        slab = self.slabs[j]
        src = wl[k0:k0 + kc * 128, c0:c0 + gw].rearrange("(k p) n -> p k n", p=128)
        self.dma("pool", slab[:, 0:kc, 0:gw], src, "slab%d" % j, w=[("slab", j)], post_barrier=(j >= 2))
        return slab, ("slab", j)

    def linear(self, wl, K, c0, ncols, rhs_fn, nblk, evac, bw=512):
        KC = K // 128
        nparts = max(1, KC // 16)
        for g0 in range(c0, c0 + ncols, 256):
            gw = min(256, c0 + ncols - g0)
            nm = gw // 128
            banks = {}
            for part in range(nparts):
                kc = min(KC, 16)
                slab, skey = self.load_slab(wl, part * 2048, kc, g0, gw)
                for mi in range(nm):
                    for blk in range(nblk):
                        if part == 0:
                            banks[(mi, blk)] = self.next_bank()
                        b = banks[(mi, blk)]
                        rk = [skey]
                        mms = []
                        for k in range(kc):
                            rap, rkeys = rhs_fn(part * 16 + k, blk)
                            rk += rkeys
                            mms.append((slab[:, k, mi * 128:(mi + 1) * 128], rap,
                                        part == 0 and k == 0, part == nparts - 1 and k == kc - 1))
                        out = self.ps[b][:, 0:bw]

                        def fn(e, out=out, mms=mms):
                            ins = None
                            for lhsT, rhs, st, sp in mms:
                                ins = e.matmul(out, lhsT=lhsT, rhs=rhs, start=st, stop=sp)
                            return ins
                        self.op("pe", fn, r=rk, w=[("ps", b)])
                        if part == nparts - 1:
                            evac((g0 - c0) // 128 + mi, blk, self.ps[b][:, 0:bw], ("ps", b))

    def linear_swapped(self, wl, c0, ncols, lhs_fn, ntch, evac):
        for g0 in range(c0, c0 + ncols, 256):
            gw = min(256, c0 + ncols - g0)
            slab, skey = self.load_slab(wl, 0, 16, g0, gw)
            for tc in range(ntch):
                b = self.next_bank()
                rk = [skey]
                mms = []
                for k in range(16):
                    lap, lkeys = lhs_fn(k, tc)
                    rk += lkeys
                    mms.append((lap, slab[:, k, 0:gw], k == 0, k == 15))
                out = self.ps[b][:, 0:gw]

                def fn(e, out=out, mms=mms):
                    ins = None
                    for lhsT, rhs, st, sp in mms:
                        ins = e.matmul(out, lhsT=lhsT, rhs=rhs, start=st, stop=sp)
                    return ins
                self.op("pe", fn, r=rk, w=[("ps", b)])
                evac(tc, g0 - c0, gw, out, ("ps", b))

    def stats_rstd(self, sq_fn, nchunks, add_const, bw=512):
        b = self.misc_bank()
        rk = ["ones"]
        mms = []
        for c in range(nchunks):
            ap, keys = sq_fn(c)
            rk += keys
            mms.append((ap, c == 0, c == nchunks - 1))
        out = self.ps[b][:, 0:bw]
        ones = self.ones_bf

        def fn(e, out=out, mms=mms, ones=ones):
            ins = None
            for rhs, st, sp in mms:
                ins = e.matmul(out, lhsT=ones[:, :], rhs=rhs, start=st, stop=sp)
            return ins
        self.op("pe", fn, r=rk, w=[("ps", b)])
        ri = self.rst_rr % len(self.rst)
        self.rst_rr += 1
        rt, rkey = self.rst[ri], ("rst", ri)
        rs = rt[:, 0:bw]
        st_, skey_ = self.work()
        sq_ = st_[:, 0:bw]
        cb = self.c_dn[:, 0:1] if add_const > 1e-3 else self.c_hd[:, 0:1]
        self.op("act", lambda e: e.activation(out=sq_, in_=out, func=AF.Sqrt, bias=cb),
                r=[("ps", b), "cconst"], w=[skey_])
        self.op("dve", lambda e: e.reciprocal(out=rs, in_=sq_), r=[skey_], w=[rkey])
        return rs, rkey

    def prenorm(self, xv, blk, hv, hkey, gvec, svec, vkey):
        cs = slice(blk * 512, (blk + 1) * 512)
        hs = cs if hv.shape[2] > 512 else slice(0, 512)
        kb = blk if hv.shape[2] > 512 else 0
        for c in range(DC):
            xin = xv[:, c, cs]
            hout = hv[:, c, hs]
            self.op("act", lambda e, xin=xin, hout=hout: e.activation(out=hout, in_=xin, func=AF.Square),
                    r=[("x", c, blk)], w=[(hkey, c, kb)])
        rs, rkey = self.stats_rstd(lambda c: (hv[:, c, hs], [(hkey, c, kb)]), DC, D * EPS)
        for c in range(DC):
            xin = xv[:, c, cs]
            hout = hv[:, c, hs]
            t, tkey = self.work()
            self.op("dve", lambda e, xin=xin, t=t: e.tensor_tensor(out=t[:, :], in0=xin, in1=rs, op=ALU.mult),
                    r=[("x", c, blk), rkey], w=[tkey])
            g1 = gvec[:, c:c + 1]
            s1 = svec[:, c:c + 1]
            self.op("act", lambda e, t=t, hout=hout, g1=g1, s1=s1: e.activation(
                out=hout, in_=t[:, :], func=AF.Identity, bias=s1, scale=g1),
                r=[tkey, vkey], w=[(hkey, c, kb)])

    def postnorm_residual(self, xv, blk, yv, ykey, sqv, sqkey, gtvec, vkey):
        cs = slice(blk * 512, (blk + 1) * 512)
        for c in range(DC):
            yin = yv[:, c, :]
            so = sqv[:, c, :]
            self.op("act", lambda e, yin=yin, so=so: e.activation(out=so, in_=yin, func=AF.Square),
                    r=[(ykey, c)], w=[(sqkey, c, 0)])
        rs, rkey = self.stats_rstd(lambda c: (sqv[:, c, :], [(sqkey, c, 0)]), DC, D * EPS)
        for c in range(DC):
            yin = yv[:, c, :]
            xio = xv[:, c, cs]
            t, tkey = self.work()
            g1 = gtvec[:, c:c + 1]
            self.op("dve", lambda e, yin=yin, t=t, g1=g1: e.scalar_tensor_tensor(
                out=t[:, :], in0=yin, scalar=g1, in1=rs, op0=ALU.mult, op1=ALU.mult),
                r=[(ykey, c), rkey, vkey], w=[tkey])
            self.op("dve", lambda e, xio=xio, t=t: e.tensor_tensor(out=xio, in0=xio, in1=t[:, :], op=ALU.add),
                    r=[tkey, ("x", c, blk)], w=[("x", c, blk)])

    def attn_accumulate(self, nchunks, st_fn, pv_fn, scale, sum_b, o_banks, bw=512, st_pool=(0, 1, 2, 3)):
        LA = 2
        pendq = []

        def emit_pv(pj, pet, pekey):
            so = self.ps[sum_b][:, 0:bw]
            ones = self.ones_bf
            rhs = pet[:, 0:bw]
            st, sp = pj == 0, pj == nchunks - 1
            self.op("pe", lambda e, so=so, rhs=rhs, st=st, sp=sp, ones=ones: e.matmul(
                so, lhsT=ones[:, :], rhs=rhs, start=st, stop=sp), r=[pekey, "ones"], w=[("ps", sum_b)])
            pvs = pv_fn(pj)
            plan = []
            rk = [pekey]
            wk = set()
            for (oi, lo, hi, vl, keys, first, last) in pvs:
                plan.append((self.ps[o_banks[oi]][:, lo:hi], vl, pet[:, lo:hi], first, last))
                rk += keys
                wk.add(("ps", o_banks[oi]))

            def fn2(e, plan=plan):
                ins = None
                for o, lhsT, rhs, st, sp in plan:
                    ins = e.matmul(o, lhsT=lhsT, rhs=rhs, start=st, stop=sp)
                return ins
            self.op("pe", fn2, r=rk, w=list(wk))

        for j in range(nchunks):
            b = self.next_bank(st_pool)
            mm = st_fn(j)
            rk = []
            for m in mm:
                rk += m[4]
            outb = self.ps[b]
            cnt = {}
            for m in mm:
                cnt[(m[0], m[1])] = cnt.get((m[0], m[1]), 0) + 1
            seen = {}
            plan = []
            for m in mm:
                key = (m[0], m[1])
                i = seen.get(key, 0)
                seen[key] = i + 1
                plan.append((outb[:, m[0]:m[1]], m[2], m[3], i == 0, i == cnt[key] - 1))

            def fn(e, plan=plan):
                ins = None
                for o, lhsT, rhs, st, sp in plan:
                    ins = e.matmul(o, lhsT=lhsT, rhs=rhs, start=st, stop=sp)
                return ins
            self.op("pe", fn, r=rk, w=[("ps", b)])
            et, ekey = self.etile()
            src = outb[:, 0:bw]
            dst = et[:, 0:bw]
            self.op("act", lambda e, src=src, dst=dst: e.activation(out=dst, in_=src, func=AF.Exp,
                                                                     scale=float(scale)),
                    r=[("ps", b)], w=[ekey])
            pendq.append((j, et, ekey))
            if len(pendq) > LA:
                emit_pv(*pendq.pop(0))
        while pendq:
            emit_pv(*pendq.pop(0))

    def recip(self, bank, bw=512):
        t, tkey = self.work()
        src = self.ps[bank][:, 0:bw]
        self.op("dve", lambda e, t=t, src=src: e.reciprocal(out=t[:, 0:bw], in_=src), r=[("ps", bank)], w=[tkey])
        return t, tkey

    def build(self):
        nc = self.nc
        depth = self.depth
        B = self
        xsT = B.din("xsT", [D, 1024])
        xpT = B.din("xpT", [D, 512])
        cvec = B.din("cvec", [128, 32])
        cache = {}
        for nm, hw in (("nak", 512), ("nav", 512), ("gk", 256), ("gv", 256), ("dk", 512), ("dv", 512)):
            cache[nm] = B.din("c_" + nm, [L_FULL, 256, hw])
        w_ada = B.din("w_ada", [L_FULL, D, 6 * D])
        w_in = B.din("w_in", [L_FULL, D, D_IN])
        w_out = B.din("w_out", [L_FULL, D, D])
        w_up = B.din("w_up", [L_FULL, D, D_FF])
        w_down = B.din("w_down", [L_FULL, D_FF, D])
        badaT = B.din("badaT", [L_FULL, 128, 96])
        vecs = B.din("vecs", [128, L_FULL * 64 + 3 * L_FULL])
        lam_in = B.din("lam_in", [128, L_FULL * 256])
        tbin = B.din("tbin", [L_FULL, 128, 4 * TB_E * 64])
        ident_in = B.din("ident_in", [128, 128])
        perm_in = B.din("perm_in", [128, 256])
        vlr_in = B.din("vlr_in", [128, 2048])
        rope_in = B.din("rope_in", [2, 128, 2048])
        ysT = B.dout("ysT", [D, 1024])
        ypT = B.dout("ypT", [D, 512])
        okv = {}
        for nm, hw in (("nak", 512), ("nav", 512), ("gk", 256), ("gv", 256), ("dk", 512), ("dv", 512)):
            okv[nm] = B.dout("o_" + nm, [2, L_FULL, 256, hw])

        with ExitStack() as es:
            def sb(name, shape, dt):
                return es.enter_context(nc.sbuf_tensor(name, list(shape), dt))
            X = sb("X", [128, 16384], F32)
            RA = sb("RA", [128, 8192], F32)
            RB = sb("RB", [128, 16384], BF16)
            RC = sb("RC", [128, 16384], BF16)
            B.slabs = [sb("slab%d" % i, [128, 16, 256], BF16) for i in range(2)]
            B.wk = [sb("wk%d" % i, [128, 512], F32) for i in range(4)]
            B.rst = [sb("rst%d" % i, [128, 512], F32) for i in range(2)]
            B.et = [sb("et%d" % i, [128, 512], BF16) for i in range(3)]
            kcs = sb("kcs", [128, 2, 512], F32)
            ident = sb("ident", [128, 128], F32)
            B.ones_bf = sb("ones_bf", [128, 128], BF16)
            ident_bf = sb("ident_bf", [128, 128], BF16)
            perm = sb("perm", [128, 256], F32)
            vlr = sb("vlr", [128, 2048], BF16)
            cv = sb("cv", [128, 32], F32)
            cvs = sb("cvs", [128, 32], BF16)
            mt = sb("mt", [128, 192], F32)
            bada = sb("bada", [128, 96], F32)
            modS = sb("modS", [128, 96], F32)
            modP = sb("modP", [128, L_FULL * 96], F32)
            vec = sb("vec", [128, L_FULL * 64 + 3 * L_FULL], F32)
            vec2 = sb("vec2", [128, 8], F32)
            lamt = sb("lamt", [128, 256], F32)
            lamw = sb("lamw", [128, 136], F32)
            B.ps = [es.enter_context(nc.psum_tensor("ps%d" % i, [128, 512], F32)) for i in range(8)]

            RAb = RA.bitcast(BF16)
            B.slabs = B.slabs + [RB[:, 8192 + i * 4096:8192 + (i + 1) * 4096].rearrange("p (k n) -> p k n", k=16)
                                 for i in range(2)]

            B.dma("sp", ident[:, :], ident_in, "c0", w=["ident"])
            B.dma("sp", perm[:, :], perm_in, "c0", w=["perm"])
            B.dma("sp", cv[:, :], cvec, "c0", w=["cv"])
            B.dma("sp", vec[:, :], vecs, "c0", w=["vec"])
            B.dma("pool", vlr[:, :], vlr_in, "c1", w=["vlr"])
            B.op("dve", lambda e: e.memset(B.ones_bf[:, :], 1.0), w=["ones"])
            B.c_dn = sb("c_dn", [128, 1], F32)
            B.c_hd = sb("c_hd", [128, 1], F32)
            B.op("dve", lambda e: e.memset(B.c_dn[:, :], D * EPS), w=["cconst"])
            B.op("dve", lambda e: e.memset(B.c_hd[:, :], HD * EPS), w=["cconst"])
            B.op("dve", lambda e: e.tensor_copy(out=ident_bf[:, :], in_=ident[:, :]), r=["ident"], w=["identbf"])
            B.op("act", lambda e: e.activation(out=cvs[:, :], in_=cv[:, :], func=AF.Silu), r=["cv"], w=["cvs"])

            def phase(is_s):
                T = 1024 if is_s else 512
                NB = T // 512
                g = 0 if is_s else 1
                xv = X[:, 0:16 * T].rearrange("p (c t) -> p c t", c=16)
                xin = xsT if is_s else xpT
                xout = ysT if is_s else ypT
                hS = RAb[:, 0:16 * T].rearrange("p (c t) -> p c t", c=16)
                yblk = RA[:, 0:8192].rearrange("p (c t) -> p c t", c=16)
                QO = RB[:, 0:16 * T].rearrange("p (c t) -> p c t", c=16)
                hblk = RB[:, 0:8192].rearrange("p (c t) -> p c t", c=16)
                TK = T + (256 if is_s else 0)
                NCH = TK // 128
                kT = RC[:, 0:4 * TK].rearrange("p (h t) -> p h t", h=4)
                vv = RC[:, 5120:5120 + NCH * 512].rearrange("p (j d) -> p j d", j=NCH)
                tab = RC[:, 10240:10240 + 5632]
                tabf = RC.bitcast(F32)[:, 5120:5120 + 2816]
                sqA = RC[:, 0:8192].rearrange("p (c t) -> p c t", c=16)
                u2 = RC[:, 0:16384].rearrange("p (c t) -> p c t", c=32)
                pst = X[:, 8192:16384]
                kf32 = pst[:, 0:2048].rearrange("p (h t) -> p h t", h=4)
                ostg = [pst[:, 2048 + i * 512:2048 + (i + 1) * 512] for i in range(4)]

                B.S.barrier()
                for c4 in range(4):
                    B.dma("sp", xv[:, 4 * c4:4 * c4 + 4, :],
                          xin[512 * c4:512 * (c4 + 1), :].rearrange("(c p) t -> p c t", p=128), "xin",
                          w=[("x", c, blk) for c in range(4 * c4, 4 * c4 + 4) for blk in range(NB)],
                          post_barrier=True)

                B.stage('xload' + ('s' if is_s else 'p'))
                for l in range(depth):
                    lam_init = 0.8 - 0.6 * math.exp(-0.3 * l)
                    if is_s:
                        B.slab_wide = True
                        B.dma("sp", bada[:, :], badaT[l], "bada", w=["bada"])
                        mb = B.misc_bank()
                        mout = B.ps[mb][:, 0:192].rearrange("p (j v) -> p j v", v=2)
                        first = [True]

                        def ada_evac(mc, blk, bank_ap, bkey):
                            pass
                        for g0 in range(0, 6 * D, 256):
                            slab, skey = B.load_slab(w_ada[l], 0, 16, g0, 256)
                            plan = []
                            for mi in range(2):
                                j = g0 // 128 + mi
                                for k in range(16):
                                    plan.append((mout[:, j, :], slab[:, k, mi * 128:(mi + 1) * 128],
                                                 cvs[:, 2 * k:2 * k + 2], k == 0, k == 15))

                            def fn(e, plan=plan):
                                ins = None
                                for o, lhsT, rhs, st, sp in plan:
                                    ins = e.matmul(o, lhsT=lhsT, rhs=rhs, start=st, stop=sp)
                                return ins
                            B.op("pe", fn, r=[skey, "cvs"], w=[("ps", mb)])
                        mtv = mt[:, :].rearrange("p (j v) -> p j v", v=2)
                        for v_ in range(2):
                            B.op("dve", lambda e, v_=v_, mout=mout: e.tensor_tensor(out=mtv[:, :, v_], in0=mout[:, :, v_],
                                                                       in1=bada[:, :], op=ALU.add),
                                 r=[("ps", mb), "bada"], w=[("mt", v_)])
                        for v_, dst, dkey in ((0, modS[:, :], "modS"), (1, modP[:, l * 96:(l + 1) * 96], ("modP", l))):
                            def mchunk(n, v_=v_):
                                return mtv[:, n * 16:(n + 1) * 16, v_]

                            def ng(i):
                                return vec[:, l * 64 + i * 16: l * 64 + (i + 1) * 16]
                            steps = [
                                (dst[:, 0:16], mchunk(1), ng(0), True),
                                (dst[:, 16:32], mchunk(0), None, False),
                                (dst[:, 32:48], mchunk(2), ng(1), False),
                                (dst[:, 48:64], mchunk(4), ng(2), True),
                                (dst[:, 64:80], mchunk(3), None, False),
                                (dst[:, 80:96], mchunk(5), ng(3), False),
                            ]
                            for (o_, m_, g_, plus1) in steps:
                                if g_ is None:
                                    B.op("dve", lambda e, o_=o_, m_=m_: e.tensor_copy(out=o_, in_=m_),
                                         r=[("mt", v_)], w=[dkey])
                                else:
                                    B.op("dve", lambda e, o_=o_, m_=m_, g_=g_, plus1=plus1: e.scalar_tensor_tensor(
                                        out=o_, in0=m_, scalar=(1.0 if plus1 else 0.0), in1=g_,
                                        op0=ALU.add, op1=ALU.mult), r=[("mt", v_), "vec"], w=[dkey])
                                    B.op("dve", lambda e, o_=o_: e.tensor_scalar(out=o_, in0=o_, scalar1=SQD,
                                                                                scalar2=None, op0=ALU.mult),
                                         r=[dkey], w=[dkey])
                    B.stage('ada' + ('s' if is_s else 'p'))
                    mod = modS[:, :] if is_s else modP[:, l * 96:(l + 1) * 96]
                    mkey = "modS" if is_s else ("modP", l)
                    gA, shA, gtA = mod[:, 0:16], mod[:, 16:32], mod[:, 32:48]
                    gM, shM, gtM = mod[:, 48:64], mod[:, 64:80], mod[:, 80:96]

                    vb = L_FULL * 64
                    B.op("dve", lambda e, l=l: e.tensor_scalar(out=vec2[:, 0:1], in0=vec[:, vb + l:vb + l + 1],
                                                          scalar1=SQH, scalar2=None, op0=ALU.mult),
                         r=["vec"], w=["vec2"])
                    B.op("dve", lambda e, l=l: e.tensor_scalar(out=vec2[:, 1:2],
                                                          in0=vec[:, vb + L_FULL + l:vb + L_FULL + l + 1],
                                                          scalar1=SQH, scalar2=None, op0=ALU.mult),
                         r=["vec"], w=["vec2"])
                    B.op("dve", lambda e, l=l, lam_init=lam_init: e.tensor_scalar(out=vec2[:, 2:3],
                                                          in0=vec[:, vb + 2 * L_FULL + l:vb + 2 * L_FULL + l + 1],
                                                          scalar1=SQH * (1.0 - lam_init), scalar2=None, op0=ALU.mult),
                         r=["vec"], w=["vec2"])
                    B.dma("sp", lamt[:, :], lam_in[:, l * 256:(l + 1) * 256], "lamt", w=["lamt"])
                    B.op("dve", lambda e: e.tensor_tensor(out=lamw[:, 0:64], in0=lamt[:, 0:64], in1=lamt[:, 64:128],
                                                          op=ALU.mult), r=["lamt"], w=["lamw"])
                    B.op("dve", lambda e: e.tensor_tensor(out=lamw[:, 64:128], in0=lamt[:, 128:192],
                                                          in1=lamt[:, 192:256], op=ALU.mult), r=["lamt"], w=["lamw"])
                    B.op("dve", lambda e: e.reduce_sum(out=lamw[:, 128:130],
                                                       in_=lamw[:, 0:128].rearrange("p (a b) -> p a b", a=2),
                                                       axis=mybir.AxisListType.X), r=["lamw"], w=["lamw2"])
                    B.op("act", lambda e: e.activation(out=lamw[:, 130:132], in_=lamw[:, 128:130], func=AF.Exp),
                         r=["lamw2"], w=["lamw3"])
                    B.op("dve", lambda e: e.tensor_tensor(out=lamw[:, 132:133], in0=lamw[:, 131:132],
                                                          in1=lamw[:, 130:131], op=ALU.subtract),
                         r=["lamw3"], w=["lamw4"])
                    B.op("dve", lambda e, lam_init=lam_init: e.tensor_scalar(out=vec2[:, 3:4], in0=lamw[:, 132:133],
                                                          scalar1=-lam_init, scalar2=None, op0=ALU.add),
                         r=["lamw4"], w=["vec2"])
                    gq1, gk1, dg1, nlam = vec2[:, 0:1], vec2[:, 1:2], vec2[:, 2:3], vec2[:, 3:4]

                    B.slab_wide = (not is_s)
                    B.S.barrier()
                    for blk in range(NB):
                        B.prenorm(xv, blk, hS, "h", gA, shA, mkey)
                    if l == 0:
                        B.tap("h0" + ("s" if is_s else "p"), hS, [("h", c, b_) for c in range(16) for b_ in range(NB)],
                              [128, 16, T])

                    B.stage('prenorm' + ('s' if is_s else 'p'))

                    def h_rhs(k, blk):
                        return hS[:, k, blk * 512:(blk + 1) * 512], [("h", k, blk)]

                    def h_lhs(k, tc):
                        return hS[:, k, tc * 128:(tc + 1) * 128], [("h", k, tc // 4)]

                    def load_ctx(knm, vnm, nh):
                        hw = nh * 128
                        B.dma("sp", kcs[:, :, 0:hw], cache[knm][l].rearrange("(j p) w -> p j w", p=128), "kcs",
                              w=["kcs"])
                        B.dma("pool", vv[:, 8:10, 0:hw], cache[vnm][l].rearrange("(j p) w -> p j w", p=128), "vctx",
                              w=[("v", 8), ("v", 9)], post_barrier=True)
                        for h0 in range(0, nh, 2):
                            mb = B.misc_bank()
                            outb = B.ps[mb]

                            def fn(e, h0=h0, outb=outb):
                                ins = None
                                for hh in range(2):
                                    for j in range(2):
                                        o = outb[:, (hh * 2 + j) * 128:(hh * 2 + j + 1) * 128]
                                        ins = e.transpose(o, kcs[:, j, (h0 + hh) * 128:(h0 + hh + 1) * 128],
                                                          ident[:, :])
                                return ins
                            B.op("pe", fn, r=["kcs", "ident"], w=[("ps", mb)])
                            dst = kT[:, h0:h0 + 2, T:T + 256]
                            src = outb[:, :].rearrange("p (h t) -> p h t", h=2)
                            B.copy_op(B.evac_eng(), dst, src, r=[("ps", mb)], w=[("kT", h0, 9), ("kT", h0 + 1, 9)])

                    def evac_plain(dst_fn, wkey_fn, scale=None, f32_fn=None):
                        def ev(mc, blk, bank_ap, bkey):
                            dst = dst_fn(mc, blk)
                            if f32_fn is not None:
                                d2, k2 = f32_fn(mc, blk)
                                B.op("dve", lambda e: e.tensor_copy(out=d2, in_=bank_ap), r=[bkey], w=k2)
                                B.op("act", lambda e: e.activation(out=dst, in_=d2, func=AF.Copy), r=k2,
                                     w=wkey_fn(mc, blk))
                                return
                            if scale is None:
                                B.copy_op(B.evac_eng(), dst, bank_ap, r=[bkey], w=wkey_fn(mc, blk))
                            else:
                                B.op("act", lambda e: e.activation(out=dst, in_=bank_ap, func=AF.Identity,
                                                                   scale=float(scale)), r=[bkey], w=wkey_fn(mc, blk))
                        return ev

                    def bs(blk):
                        return slice(blk * 512, (blk + 1) * 512)

                    def v_evac(nh, out_nm):
                        def ev(tc, coff, gw, bank_ap, bkey):
                            dst = vv[:, tc, coff:coff + gw]
                            if is_s:
                                B.copy_op(B.evac_eng(), dst, bank_ap, r=[bkey], w=[("v", tc)])
                            else:
                                i = B.ostg_rr % 4
                                B.ostg_rr += 1
                                st = ostg[i]
                                B.op("dve", lambda e: e.tensor_copy(out=st[:, 0:gw], in_=bank_ap),
                                     r=[bkey], w=[("ostg", i)])
                                B.op("act", lambda e: e.activation(out=dst, in_=st[:, 0:gw], func=AF.Copy),
                                     r=[("ostg", i)], w=[("v", tc)])
                                seq, tr = tc // 2, (tc % 2) * 128
                                B.dma("sp", okv[out_nm][seq, l, tr:tr + 128, coff:coff + gw], st[:, 0:gw],
                                      "ostg%d" % i, r=[("ostg", i)])
                        return ev

                    def k_out(nh, out_nm, src_f32, skey_fn):
                        for tc in range(4):
                            mb = B.misc_bank()
                            outb = B.ps[mb]

                            def fn(e, tc=tc, outb=outb):
                                ins = None
                                for h in range(nh):
                                    ins = e.transpose(outb[:, h * 128:(h + 1) * 128],
                                                      src_f32[:, h, tc * 128:(tc + 1) * 128], ident[:, :])
                                return ins
                            B.op("pe", fn, r=[skey_fn(h) for h in range(nh)] + ["ident"], w=[("ps", mb)])
                            i = B.ostg_rr % 4
                            B.ostg_rr += 1
                            st = ostg[i]
                            hw = nh * 128
                            B.copy_op(B.evac_eng(), st[:, 0:hw], outb[:, 0:hw], r=[("ps", mb)], w=[("ostg", i)])
                            seq, tr = tc // 2, (tc % 2) * 128
                            B.dma("sp", okv[out_nm][seq, l, tr:tr + 128, :], st[:, 0:hw], "ostg%d" % i,
                                  r=[("ostg", i)])

                    def mul_segs(dst, segs, R, rkey, wkeys):
                        for (bk, lo, hi) in segs:
                            B.op("dve", lambda e, bk=bk, lo=lo, hi=hi: e.tensor_tensor(
                                out=dst[:, lo:hi], in0=B.ps[bk][:, lo:hi], in1=R[:, lo:hi], op=ALU.mult),
                                r=[("ps", bk), rkey], w=wkeys)

                    def finish_head(sum_b, o_bs, slot, blk, bw=512):
                        R, rkey = B.recip(sum_b)
                        dst = QO[:, slot, bs(blk)]
                        segs = [(o_bs[0], 0, 512)] if len(o_bs) == 1 else [(o_bs[0], 0, 256), (o_bs[1], 256, 512)]
                        mul_segs(dst, segs, R, rkey, [("qo", slot, blk)])

                    acc_rr = [0]

                    def acc_pair():
                        a = acc_rr[0] % 2
                        acc_rr[0] += 1
                        return (4, 5) if a == 0 else (6, 7)

                    if is_s:
                        B.dma("pool", tab[:, :], tbin[l], "tab", w=["tab"], post_barrier=True)
                        load_ctx("nak", "nav", 4)
                    B.linear(w_in[l], D, C_NAQ, 512, h_rhs, NB,
                             evac_plain(lambda mc, blk: QO[:, mc, bs(blk)], lambda mc, blk: [("qo", mc, blk)],
                                        scale=SCALE_H))
                    B.linear(w_in[l], D, C_NAK, 512, h_rhs, NB,
                             evac_plain(lambda mc, blk: kT[:, mc, bs(blk)], lambda mc, blk: [("kT", mc, blk)],
                                        f32_fn=None if is_s else (lambda mc, blk: (kf32[:, mc, :], [("kf", mc)]))))
                    B.linear_swapped(w_in[l], C_NAV, 512, h_lhs, T // 128, v_evac(4, "nav"))
                    if not is_s:
                        k_out(4, "nak", kf32, lambda h: ("kf", h))
                    B.stage('na_proj' + ('s' if is_s else 'p'))
                    if is_s:
                        for h in range(4):
                            for blk in range(2):
                                chunks = list(range(0, 6)) if blk == 0 else list(range(2, 8))
                                chunks = chunks + [8, 9]
                                sum_b, o_b = acc_pair()

                                def st_fn(j, h=h, blk=blk, chunks=chunks):
                                    i = chunks[j]
                                    mm = [(0, 512, kT[:, h, i * 128:(i + 1) * 128], QO[:, h, bs(blk)],
                                           [("kT", h, i // 4 if i < 8 else 9), ("qo", h, blk)])]
                                    if i < 8:
                                        e0 = 10 - 2 * i + 8 * blk
                                        mm.append((0, 512, vlr[0:96, i * 128:(i + 1) * 128],
                                                   vlr[0:96, 1024 + blk * 512:1024 + (blk + 1) * 512], ["vlr"]))
                                        mm.append((0, 512, ident_bf[:, :],
                                                   tab[:, h * TB_E * 64 + e0 * 64:h * TB_E * 64 + (e0 + 8) * 64],
                                                   ["tab", "identbf"]))
                                    return mm

                                def pv_fn(j, h=h, chunks=chunks, o_b=o_b):
                                    i = chunks[j]
                                    return [(0, 0, 512, vv[:, i, h * 128:(h + 1) * 128], [("v", i)],
                                             j == 0, j == len(chunks) - 1)]
                                B.attn_accumulate(len(chunks), st_fn, pv_fn, 1.0, sum_b, [o_b])
                                finish_head(sum_b, [o_b], h, blk)
                    else:
                        for h in range(4):
                            sum_b, o_b, o_b2 = 4, 5, 6

                            def st_fn(j, h=h):
                                return [(0, 256, kT[:, h, j * 128:(j + 1) * 128], QO[:, h, 0:256],
                                         [("kT", h, 0), ("qo", h, 0)]),
                                        (256, 512, kT[:, h, 256 + j * 128:256 + (j + 1) * 128], QO[:, h, 256:512],
                                         [("kT", h, 0), ("qo", h, 0)])]

                            def pv_fn(j, h=h):
                                return [(0, 0, 256, vv[:, j, h * 128:(h + 1) * 128], [("v", j)], j == 0, j == 1),
                                        (1, 256, 512, vv[:, 2 + j, h * 128:(h + 1) * 128], [("v", 2 + j)],
                                         j == 0, j == 1)]
                            B.attn_accumulate(2, st_fn, pv_fn, 1.0, sum_b, [o_b, o_b2], st_pool=(0, 1, 2, 3))
                            finish_head(sum_b, [o_b, o_b2], h, 0)
                    if l == 0:
                        B.tap("ona" + ("s" if is_s else "p"), QO[:, 0:4, :],
                              [("qo", h, b_) for h in range(4) for b_ in range(NB)], [128, 4, T])

                    B.stage('na_attn' + ('s' if is_s else 'p'))
                    def normrope_evac(dst_fn, wkey_fn, gain, do_norm, rope_idx, f32_fn=None):
                        def ev(mc, blk, bank_ap, bkey):
                            dst = dst_fn(mc, blk)
                            wk_ = wkey_fn(mc, blk)
                            q32, qkey = B.work()
                            if do_norm:
                                et, ekey = B.etile()
                                B.op("dve", lambda e: e.tensor_copy(out=q32[:, :], in_=bank_ap), r=[bkey], w=[qkey])
                                B.op("act", lambda e: e.activation(out=et[:, :], in_=q32[:, :], func=AF.Square),
                                     r=[qkey], w=[ekey])
                                rs, rkey = B.stats_rstd(lambda c: (et[:, :], [ekey]), 1, HD * EPS)
                                need32 = (rope_idx is not None) or (f32_fn is not None)
                                tgt = q32[:, :] if need32 else dst
                                B.op("dve", lambda e: e.scalar_tensor_tensor(out=tgt, in0=q32[:, :], scalar=gain,
                                                                             in1=rs, op0=ALU.mult, op1=ALU.mult),
                                     r=[qkey, rkey, "vec2"], w=[qkey] if need32 else wk_)
                                if not need32:
                                    return
                            else:
                                B.copy_op("act", q32[:, :], bank_ap, r=[bkey], w=[qkey])
                            if f32_fn is not None:
                                d2, k2 = f32_fn(mc, blk)
                                B.op("act", lambda e: e.activation(out=d2, in_=q32[:, :], func=AF.Copy),
                                     r=[qkey], w=k2)
                            if rope_idx is None:
                                B.copy_op("dve", dst, q32[:, :], r=[qkey], w=wk_)
                                return
                            mb = B.misc_bank()
                            rb = B.ps[mb][:, 0:512]
                            pm = perm[:, rope_idx * 128:(rope_idx + 1) * 128]
                            B.op("pe", lambda e: e.matmul(rb, lhsT=pm, rhs=q32[:, :], start=True, stop=True),
                                 r=[qkey, "perm"], w=[("ps", mb)])
                            cosv = tabf[:, blk * 512:(blk + 1) * 512]
                            sinv = tabf[:, 1024 + blk * 512:1024 + (blk + 1) * 512]
                            t1, t1k = B.work()
                            t2, t2k = B.work()
                            B.op("dve", lambda e: e.tensor_tensor(out=t1[:, :], in0=q32[:, :], in1=cosv, op=ALU.mult),
                                 r=[qkey, "tab"], w=[t1k])
                            B.op("dve", lambda e: e.tensor_tensor(out=t2[:, :], in0=rb, in1=sinv, op=ALU.mult),
                                 r=[("ps", mb), "tab"], w=[t2k])
                            B.op("dve", lambda e: e.tensor_tensor(out=dst, in0=t1[:, :], in1=t2[:, :], op=ALU.add),
                                 r=[t1k, t2k], w=wk_)
                        return ev

                    if is_s:
                        B.dma("sp", tabf[:, 0:2048], rope_in[0], "tab", w=["tab"], r=[])
                        load_ctx("gk", "gv", 2)
                    B.linear(w_in[l], D, C_GQ, 1024, h_rhs, NB,
                             normrope_evac(lambda mc, blk: QO[:, 4 + mc, bs(blk)],
                                           lambda mc, blk: [("qo", 4 + mc, blk)], gq1, True, 0 if is_s else None))
                    B.linear(w_in[l], D, C_GK, 256, h_rhs, NB,
                             normrope_evac(lambda mc, blk: kT[:, mc, bs(blk)], lambda mc, blk: [("kT", mc, blk)],
                                           gk1, True, 0 if is_s else None,
                                           f32_fn=None if is_s else (lambda mc, blk: (kf32[:, mc, :], [("kf", mc)]))))
                    B.linear_swapped(w_in[l], C_GV, 256, h_lhs, T // 128, v_evac(2, "gv"))
                    if not is_s:
                        k_out(2, "gk", kf32, lambda h: ("kf", h))
                    B.stage('gqa_proj' + ('s' if is_s else 'p'))
                    for qh in range(8):
                        kvh = qh // 4
                        if is_s:
                            for blk in range(2):
                                sum_b, o_b = acc_pair()

                                def st_fn(j, qh=qh, kvh=kvh, blk=blk):
                                    return [(0, 512, kT[:, kvh, j * 128:(j + 1) * 128], QO[:, 4 + qh, bs(blk)],
                                             [("kT", kvh, j // 4 if j < 8 else 9), ("qo", 4 + qh, blk)])]

                                def pv_fn(j, kvh=kvh):
                                    return [(0, 0, 512, vv[:, j, kvh * 128:(kvh + 1) * 128], [("v", j)],
                                             j == 0, j == 9)]
                                B.attn_accumulate(10, st_fn, pv_fn, SCALE_H, sum_b, [o_b])
                                finish_head(sum_b, [o_b], 4 + qh, blk)
                        else:
                            sum_b, o_b, o_b2 = 4, 5, 6

                            def st_fn(j, qh=qh, kvh=kvh):
                                return [(0, 256, kT[:, kvh, j * 128:(j + 1) * 128], QO[:, 4 + qh, 0:256],
                                         [("kT", kvh, 0), ("qo", 4 + qh, 0)]),
                                        (256, 512, kT[:, kvh, 256 + j * 128:256 + (j + 1) * 128],
                                         QO[:, 4 + qh, 256:512], [("kT", kvh, 0), ("qo", 4 + qh, 0)])]

                            def pv_fn(j, kvh=kvh):
                                return [(0, 0, 256, vv[:, j, kvh * 128:(kvh + 1) * 128], [("v", j)], j == 0, j == 1),
                                        (1, 256, 512, vv[:, 2 + j, kvh * 128:(kvh + 1) * 128], [("v", 2 + j)],
                                         j == 0, j == 1)]
                            B.attn_accumulate(2, st_fn, pv_fn, SCALE_H, sum_b, [o_b, o_b2])
                            finish_head(sum_b, [o_b, o_b2], 4 + qh, 0)
                    if l == 0:
                        B.tap("ogq" + ("s" if is_s else "p"), QO[:, 4:12, :],
                              [("qo", h, b_) for h in range(4, 12) for b_ in range(NB)], [128, 8, T])

                    B.stage('gqa_attn' + ('s' if is_s else 'p'))
                    if is_s:
                        B.dma("sp", tabf[:, 0:2048], rope_in[1], "tab", w=["tab"])
                        load_ctx("dk", "dv", 4)
                    B.linear(w_in[l], D, C_DQ, 512, h_rhs, NB,
                             normrope_evac(lambda mc, blk: QO[:, 12 + mc, bs(blk)],
                                           lambda mc, blk: [("qo", 12 + mc, blk)], None, False, 1 if is_s else None))
                    B.linear(w_in[l], D, C_DK, 512, h_rhs, NB,
                             normrope_evac(lambda mc, blk: kT[:, mc, bs(blk)], lambda mc, blk: [("kT", mc, blk)],
                                           None, False, 1 if is_s else None,
                                           f32_fn=None if is_s else (lambda mc, blk: (kf32[:, mc, :], [("kf", mc)]))))
                    B.linear_swapped(w_in[l], C_DV, 512, h_lhs, T // 128, v_evac(4, "dv"))
                    if not is_s:
                        k_out(4, "dk", kf32, lambda h: ("kf", h))
                    B.stage('diff_proj' + ('s' if is_s else 'p'))
                    for h in range(4):
                        for blk in range(NB):
                            nchk = 10 if is_s else 2
                            for mp in range(2):
                                ps_ = slice(mp * 64, (mp + 1) * 64)
                                if is_s:
                                    sum_b, o_bs = ((4, [5]) if mp == 0 else (6, [7]))
                                else:
                                    sum_b, o_bs = ((2, [3, 4]) if mp == 0 else (5, [6, 7]))
                                if is_s:
                                    def st_fn(j, h=h, blk=blk, ps_=ps_):
                                        return [(0, 512, kT[ps_, h, j * 128:(j + 1) * 128], QO[ps_, 12 + h, bs(blk)],
                                                 [("kT", h, j // 4 if j < 8 else 9), ("qo", 12 + h, blk)])]

                                    def pv_fn(j, h=h):
                                        return [(0, 0, 512, vv[:, j, h * 128:(h + 1) * 128], [("v", j)],
                                                 j == 0, j == 9)]
                                else:
                                    def st_fn(j, h=h, ps_=ps_):
                                        return [(0, 256, kT[ps_, h, j * 128:(j + 1) * 128], QO[ps_, 12 + h, 0:256],
                                                 [("kT", h, 0), ("qo", 12 + h, 0)]),
                                                (256, 512, kT[ps_, h, 256 + j * 128:256 + (j + 1) * 128],
                                                 QO[ps_, 12 + h, 256:512], [("kT", h, 0), ("qo", 12 + h, 0)])]

                                    def pv_fn(j, h=h):
                                        return [(0, 0, 256, vv[:, j, h * 128:(h + 1) * 128], [("v", j)],
                                                 j == 0, j == 1),
                                                (1, 256, 512, vv[:, 2 + j, h * 128:(h + 1) * 128], [("v", 2 + j)],
                                                 j == 0, j == 1)]
                                B.attn_accumulate(nchk, st_fn, pv_fn, SCALE_DQ, sum_b, o_bs,
                                                  st_pool=(0, 1, 2, 3) if is_s else (0, 1))
                            R0, r0k = B.recip(4 if is_s else 2)
                            R1, r1k = B.recip(6 if is_s else 5)
                            t0, t0k = B.work()
                            t1, t1k = B.work()
                            if is_s:
                                segs0, segs1 = [(5, 0, 512)], [(7, 0, 512)]
                            else:
                                segs0, segs1 = [(3, 0, 256), (4, 256, 512)], [(6, 0, 256), (7, 256, 512)]
                            mul_segs(t0, segs0, R0, r0k, [t0k])
                            mul_segs(t1, segs1, R1, r1k, [t1k])
                            B.op("dve", lambda e, t0=t0, t1=t1: e.scalar_tensor_tensor(
                                out=t0[:, :], in0=t1[:, :], scalar=nlam, in1=t0[:, :], op0=ALU.mult, op1=ALU.add),
                                r=[t0k, t1k, "vec2"], w=[t0k])
                            et, ekey = B.etile()
                            B.op("act", lambda e, et=et, t0=t0: e.activation(out=et[:, :], in_=t0[:, :],
                                                                              func=AF.Square), r=[t0k], w=[ekey])
                            rs, rkey = B.stats_rstd(lambda c, et=et, ekey=ekey: (et[:, :], [ekey]), 1, HD * EPS)
                            dst = QO[:, 12 + h, bs(blk)]
                            B.op("dve", lambda e, t0=t0, rs=rs, dst=dst: e.scalar_tensor_tensor(
                                out=dst, in0=t0[:, :], scalar=dg1, in1=rs, op0=ALU.mult, op1=ALU.mult),
                                r=[t0k, rkey, "vec2"], w=[("qo", 12 + h, blk)])
                    if l == 0:
                        B.tap("odf" + ("s" if is_s else "p"), QO[:, 12:16, :],
                              [("qo", h, b_) for h in range(12, 16) for b_ in range(NB)], [128, 4, T])

                    B.stage('diff_attn' + ('s' if is_s else 'p'))
                    B.S.barrier()
                    for blk in range(NB):
                        def o_rhs(k, blk_, blk=blk):
                            return QO[:, k, bs(blk)], [("qo", k, blk)]

                        def y_evac(mc, blk_, bank_ap, bkey):
                            B.copy_op(B.evac_eng(), yblk[:, mc, :], bank_ap, r=[bkey], w=[("y", mc)])
                        B.linear(w_out[l], D, 0, D, o_rhs, 1, y_evac)
                        B.postnorm_residual(xv, blk, yblk, "y", sqA, "sqA", gtA, mkey)
                    if l == 0:
                        B.tap("x1" + ("s" if is_s else "p"), xv, [("x", c, b_) for c in range(16) for b_ in range(NB)],
                              [128, 16, T])

                    B.stage('wout' + ('s' if is_s else 'p'))
                    B.S.barrier()
                    B.slab_wide = True
                    for blk in range(NB):
                        B.prenorm(xv, blk, hblk, "hb", gM, shM, mkey)

                        def hb_rhs(k, blk_):
                            return hblk[:, k, :], [("hb", k, 0)]
                        for half in range(2):
                            def u_evac(mc, blk_, bank_ap, bkey):
                                dst = u2[:, mc, :]
                                tr_, trk_ = B.work()
                                B.op("dve", lambda e: e.tensor_scalar(
                                    out=tr_[:, :], in0=bank_ap, scalar1=0.0, scalar2=None, op0=ALU.max),
                                    r=[bkey], w=[trk_])
                                B.op("act", lambda e: e.activation(out=dst, in_=tr_[:, :], func=AF.Square),
                                     r=[trk_], w=[("u2", mc)])
                            B.linear(w_up[l], D, half * 4096, 4096, hb_rhs, 1, u_evac)

                            def u_rhs(k, blk_):
                                return u2[:, k, :], [("u2", k)]

                            def yd_evac(mc, blk_, bank_ap, bkey, half=half):
                                if half == 0:
                                    B.copy_op("act", yblk[:, mc, :], bank_ap, r=[bkey], w=[("y", mc)])
                                else:
                                    yo = yblk[:, mc, :]
                                    B.op("dve", lambda e: e.tensor_tensor(out=yo, in0=bank_ap, in1=yo, op=ALU.add),
                                         r=[bkey, ("y", mc)], w=[("y", mc)])
                            B.linear(w_down[l, half * 4096:(half + 1) * 4096, :], 4096, 0, D, u_rhs, 1, yd_evac)
                        B.postnorm_residual(xv, blk, yblk, "y", hblk, "hb", gtM, mkey)

                for c4 in range(4):
                    B.dma("sp", xout[512 * c4:512 * (c4 + 1), :].rearrange("(c p) t -> p c t", p=128),
                          xv[:, 4 * c4:4 * c4 + 4, :], "xout",
                          r=[("x", c, blk) for c in range(4 * c4, 4 * c4 + 4) for blk in range(NB)])

            try:
                phase(True)
                phase(False)
            except _Stop:
                pass

            with ExitStack() as es2:
                eng_sems = {k: es2.enter_context(nc.semaphore("sem_" + k)) for k in Sched.STREAMS}
                dma_sems = {k: es2.enter_context(nc.semaphore("dsem_" + str(i)))
                            for i, k in enumerate(sorted(self.dma_keys))}
                block = es2.enter_context(nc.Block())
                B.S.emit(nc, {"pe": block.tensor, "act": block.scalar, "dve": block.vector,
                              "pool": block.gpsimd, "sp": block.sync}, eng_sems, dma_sems)
        return nc


def _consts():
    ident = np.eye(128, dtype=np.float32)
    perm = np.zeros((128, 256), dtype=np.float32)
    for m in range(128):
        pg = m + 32 if (m % 64) < 32 else m - 32
        perm[pg, m] = 1.0
        pd = m + 16 if (m % 32) < 16 else m - 16
        perm[pd, 128 + m] = 1.0
    t = np.arange(1024)
    row = (t // GRID_W).astype(np.float32)
    col = (t % GRID_W).astype(np.float32)
    rope = np.zeros((2, 128, 2048), dtype=np.float32)
    for p in range(128):
        i = p % 32
        inv = np.float32(10000.0) ** (-np.float32(i) / np.float32(32))
        pos = row if p < 64 else col
        ang = (pos * inv).astype(np.float32)
        sgn = -1.0 if (p % 64) < 32 else 1.0
        rope[0, p, 0:1024] = np.cos(ang)
        rope[0, p, 1024:2048] = sgn * np.sin(ang)
        i = p % 16
        inv = np.float32(10000.0) ** (-np.float32(i) / np.float32(16))
        pos = row if (p % 64) < 32 else col
        ang = (pos * inv).astype(np.float32)
        sgn = -1.0 if (p % 32) < 16 else 1.0
        rope[1, p, 0:1024] = np.cos(ang)
        rope[1, p, 1024:2048] = sgn * np.sin(ang)
    vl = np.zeros((128, 1024), dtype=np.float32)
    vr = np.zeros((128, 1024), dtype=np.float32)
    s = np.arange(1024)
    rs, cs = s // 64, s % 64
    for k in range(16):
        vl[k, :] = (rs == k)
    for k in range(64):
        vl[16 + k, :] = (cs == k)
    rq, cq = rs, cs
    r0 = np.clip(rq - 4, 0, 8)
    ws = np.clip(cq - 8, 0, 48)
    for k in range(16):
        vr[k, :] = np.where((k >= r0) & (k < r0 + 8), 0.0, NEG)
    for k in range(64):
        vr[16 + k, :] = np.where((k >= ws) & (k < ws + 16), 0.0, NEG)
    vlr = np.concatenate([vl, vr], axis=1)
    return ident, perm, rope, vlr


def _tb_index():
    a = (np.arange(128) // 64)[:, None, None]
    cs = (np.arange(128) % 64)[:, None, None]
    e = np.arange(TB_E)[None, :, None]
    cq = np.arange(64)[None, None, :]
    d = a + 17 - e + 0 * cq
    dc = cs - cq + 15 + 0 * e
    ok = (d >= 0) & (d <= 14) & (dc >= 0) & (dc <= 30)
    return np.clip(d, 0, 14), np.clip(dc, 0, 30), ok


_CACHE = {}


def _get_program(depth, taps=(), stop=None):
    key = (depth, tuple(taps), stop)
    if key not in _CACHE:
        b = Builder(depth, taps, stop)
        nc = b.build()
        _CACHE[key] = (nc, b)
    return _CACHE[key]


def kernel(x_prompt, x_sample, c, cache_na_k, cache_na_v, cache_gqa_k, cache_gqa_v, cache_diff_k, cache_diff_v,
           c_ctx, w_ada, b_ada, norm_g, w_in, w_out, na_rpb, gqa_q_g, gqa_k_g, diff_lam, diff_g, w_up, w_down,
           _depth=L_FULL, _taps=(), _stop=None):
    f = lambda a: np.ascontiguousarray(np.asarray(a, dtype=np.float32))
    x_prompt, x_sample, c = f(x_prompt), f(x_sample), f(c)
    c_ctx = f(c_ctx)
    w_ada, w_in, w_out, w_up, w_down = f(w_ada), f(w_in), f(w_out), f(w_up), f(w_down)
    b_ada, norm_g, na_rpb = f(b_ada), f(norm_g), f(na_rpb)
    gqa_q_g, gqa_k_g, diff_lam, diff_g = f(gqa_q_g), f(gqa_k_g), f(diff_lam), f(diff_g)
    caches = {"nak": f(cache_na_k), "nav": f(cache_na_v), "gk": f(cache_gqa_k), "gv": f(cache_gqa_v),
              "dk": f(cache_diff_k), "dv": f(cache_diff_v)}
    nc, bld = _get_program(_depth, _taps, _stop)
    ident, perm, rope, vlr = _consts()
    didx, cidx, ok = _tb_index()
    L = L_FULL
    tb = np.zeros((L, 128, 4, TB_E, 64), dtype=np.float32)
    for h in range(4):
        gth = na_rpb[:, h][:, didx, cidx]
        tb[:, :, h] = np.where(ok[None], gth, np.float32(0.0))
    tb = np.ascontiguousarray(tb.reshape(L, 128, 4 * TB_E * 64))
    badaT = np.ascontiguousarray(b_ada.reshape(L, 96, 128).transpose(0, 2, 1))
    vecs = np.zeros((128, L * 64 + 3 * L), dtype=np.float32)
    vecs[:, 0:L * 64] = norm_g.reshape(L, 4, 16, 128).transpose(3, 0, 1, 2).reshape(128, L * 64)
    vecs[:, L * 64:L * 64 + L] = gqa_q_g.T
    vecs[:, L * 64 + L:L * 64 + 2 * L] = gqa_k_g.T
    vecs[:, L * 64 + 2 * L:L * 64 + 3 * L] = diff_g.T
    lam_in = np.ascontiguousarray(np.broadcast_to(diff_lam.reshape(1, L * 256), (128, L * 256)))
    in_maps = []
    for i in range(8):
        cvec = np.zeros((128, 16, 2), dtype=np.float32)
        cvec[:, :, 0] = c[i].reshape(16, 128).T
        cvec[:, :, 1] = c_ctx.reshape(16, 128).T
        xp = np.concatenate([x_prompt[2 * i], x_prompt[2 * i + 1]], axis=0)
        m = {
            "xsT": np.ascontiguousarray(x_sample[i].T), "xpT": np.ascontiguousarray(xp.T),
            "cvec": np.ascontiguousarray(cvec.reshape(128, 32)),
            "w_ada": w_ada, "w_in": w_in, "w_out": w_out, "w_up": w_up, "w_down": w_down,
            "badaT": badaT, "vecs": vecs, "lam_in": lam_in, "tbin": tb,
            "ident_in": ident, "perm_in": perm, "vlr_in": vlr, "rope_in": rope,
        }
        for nm, arr in caches.items():
            m["c_" + nm] = np.ascontiguousarray(arr[i].reshape(L, 256, -1))
        in_maps.append(m)
    res = run_bass_kernel_spmd(nc, in_maps, core_ids=list(range(8)))
    R = res.results
    y_prompt = np.zeros((16, 256, D), dtype=np.float32)
    y_sample = np.zeros((8, 1024, D), dtype=np.float32)
    outs = {nm: np.zeros((16, L, 256, hw), dtype=np.float32)
            for nm, hw in (("nak", 512), ("nav", 512), ("gk", 256), ("gv", 256), ("dk", 512), ("dv", 512))}
    for i in range(8):
        y_sample[i] = R[i]["ysT"].T
        yp = R[i]["ypT"].T
        y_prompt[2 * i] = yp[0:256]
        y_prompt[2 * i + 1] = yp[256:512]
        for nm in outs:
            outs[nm][2 * i:2 * i + 2] = R[i]["o_" + nm]
    if _taps:
        kernel.last_taps = [{k: R[i]["tap_" + k] for k in bld.tap_outs} for i in range(8)]
    return (y_prompt, y_sample,
            outs["nak"].reshape(16, L, 256, 4, 128), outs["nav"].reshape(16, L, 256, 4, 128),
            outs["gk"].reshape(16, L, 256, 2, 128), outs["gv"].reshape(16, L, 256, 2, 128),
            outs["dk"].reshape(16, L, 256, 4, 128), outs["dv"].reshape(16, L, 256, 4, 128))
```
